# Optimizing a Trainium2 kernel written in Bass

```python
import jax, jax.numpy as jnp
from jax import lax
import numpy as np

D_MODEL = 2048
BATCH = 4
SEQ = 4096
DEPTH = 2

HEAD_DIM = 128
HA = 8
GA = 2
RA = HA // GA
HB = 8
WIN = 128
BLK = 128
GRID_W = 64
NA_KH = 8
NA_KW = 16
ROT_DIM = HEAD_DIM // 4
ROPE_THETA = 500000.0
D_FF = 5632
CONV_W = 3
EPS = 1e-6
NEG_INF = -1e30
QA_W = HA * HEAD_DIM
KA_W = GA * HEAD_DIM
QB_W = HB * HEAD_DIM
IN_SIZES = (QA_W, KA_W, KA_W, QB_W, QB_W, QB_W, D_MODEL, D_MODEL)
IN_COLS = QA_W + 2 * KA_W + 3 * QB_W + 2 * D_MODEL

kernel_name = "hybrid_window_gqa_neighbourhood_convffn_adaln"


def rmsnorm(x, g):
    xf = x.astype(jnp.float32)
    y = xf * lax.rsqrt(jnp.mean(xf * xf, axis=-1, keepdims=True) + EPS)
    return (y * g.astype(jnp.float32)).astype(x.dtype)


def split_cols(p):
    pts, acc = [], 0
    for s in IN_SIZES[:-1]:
        acc += s
        pts.append(acc)
    return jnp.split(p, pts, axis=-1)


def rotary_partial(x, pos):
    half = ROT_DIM // 2
    inv = jnp.float32(ROPE_THETA) ** (-jnp.arange(0, ROT_DIM, 2, dtype=jnp.float32) / ROT_DIM)
    ang = pos[:, None] * inv[None, :]
    cos = jnp.cos(ang)[None, :, None, :].astype(x.dtype)
    sin = jnp.sin(ang)[None, :, None, :].astype(x.dtype)
    x1, x2, xp = x[..., :half], x[..., half:ROT_DIM], x[..., ROT_DIM:]
    return jnp.concatenate([x1 * cos - x2 * sin, x2 * cos + x1 * sin, xp], axis=-1)


def window_attention(q, k, v, sink):
    B, S = q.shape[0], q.shape[1]
    nb = S // BLK
    qb = q.astype(jnp.float32).reshape(B, nb, BLK, GA, RA, HEAD_DIM)

    def band(t):
        tp = jnp.pad(t.astype(jnp.float32), ((0, 0), (BLK, BLK), (0, 0), (0, 0)))
        tp = tp.reshape(B, nb + 2, BLK, GA, HEAD_DIM)
        return jnp.concatenate([tp[:, :-2], tp[:, 1:-1], tp[:, 2:]], axis=2)

    kw, vw = band(k), band(v)
    qi = jnp.arange(BLK)
    ki = jnp.arange(3 * BLK) - BLK
    rel = ki[None, :] - qi[:, None]
    kpos = jnp.arange(nb)[:, None] * BLK + ki[None, :]
    mask = (jnp.abs(rel) <= WIN)[None] & ((kpos >= 0) & (kpos < S))[:, None, :]
    s = jnp.einsum('bnqgrd,bnkgd->bgrnqk', qb, kw) * (HEAD_DIM ** -0.5)
    s = jnp.where(mask[None, None, None], s, NEG_INF)
    snk = sink.astype(jnp.float32).reshape(GA, RA)[None, :, :, None, None, None]
    m = jnp.maximum(jnp.max(s, axis=-1, keepdims=True), snk)
    p = jnp.exp(s - m)
    p = p / (jnp.sum(p, axis=-1, keepdims=True) + jnp.exp(snk - m))
    o = jnp.einsum('bgrnqk,bnkgd->bnqgrd', p, vw)
    return o.reshape(B, S, HA * HEAD_DIM).astype(q.dtype)


def neighbourhood_attention(q, k, v, bias_tab):
    B, S = q.shape[0], q.shape[1]
    rows = S // GRID_W
    kh = min(NA_KH, rows)
    r = jnp.arange(rows)
    row_start = jnp.clip(r - kh // 2, 0, rows - kh)
    key_rows = row_start[:, None] + jnp.arange(kh)[None, :]
    qg = q.astype(jnp.float32).reshape(B, rows, GRID_W, HB, HEAD_DIM)
    kg = k.astype(jnp.float32).reshape(B, rows, GRID_W, HB, HEAD_DIM)[:, key_rows]
    vg = v.astype(jnp.float32).reshape(B, rows, GRID_W, HB, HEAD_DIM)[:, key_rows]
    s = jnp.einsum('brqhd,brikhd->bhrqik', qg, kg) * (HEAD_DIM ** -0.5)
    col = jnp.arange(GRID_W)
    col_start = jnp.clip(col - NA_KW // 2, 0, GRID_W - NA_KW)
    col_mask = (col[None, :] >= col_start[:, None]) & (col[None, :] < col_start[:, None] + NA_KW)
    dr_idx = key_rows - r[:, None] + (NA_KH - 1)
    dc_idx = jnp.clip(col[None, :] - col[:, None] + (NA_KW - 1), 0, 2 * NA_KW - 2)
    bias = bias_tab.astype(jnp.float32)[:, dr_idx[:, None, :, None], dc_idx[None, :, None, :]]
    s = jnp.where(col_mask[None, None, None, :, None, :], s + bias[None], NEG_INF)
    p = jax.nn.softmax(s.reshape(B, HB, rows, GRID_W, kh * GRID_W), axis=-1)
    p = p.reshape(B, HB, rows, GRID_W, kh, GRID_W)
    o = jnp.einsum('bhrqik,brikhd->brqhd', p, vg)
    return o.reshape(B, S, HB * HEAD_DIM).astype(q.dtype)


def depthwise_conv(u, w, b):
    up = jnp.pad(u, ((0, 0), (1, 1), (0, 0)))
    return up[:, :-2] * w[0] + up[:, 1:-1] * w[1] + up[:, 2:] * w[2] + b


def setup_inputs(seed: int = 0) -> dict:
    key = jax.random.key(seed)
    ks = jax.random.split(key, 22)

    def nrm(k, shape, scale):
        return jax.random.normal(k, shape, jnp.float32) * scale

    L, D = DEPTH, D_MODEL
    return {
        "x": nrm(ks[0], (BATCH, SEQ, D), 1.0),
        "c": nrm(ks[1], (BATCH, D), 1.0),
        "ada_w": nrm(ks[2], (L, D, 6 * D), 0.5 * D ** -0.5),
        "ada_b": nrm(ks[3], (L, 6 * D), 0.02),
        "norm_mix": 1.0 + nrm(ks[4], (L, D), 0.05),
        "norm_ffn": 1.0 + nrm(ks[5], (L, D), 0.05),
        "w_in": nrm(ks[6], (L, D, IN_COLS), D ** -0.5),
        "qn_a": 1.0 + nrm(ks[7], (L, HEAD_DIM), 0.05),
        "kn_a": 1.0 + nrm(ks[8], (L, HEAD_DIM), 0.05),
        "qn_b": 1.0 + nrm(ks[9], (L, HEAD_DIM), 0.05),
        "kn_b": 1.0 + nrm(ks[10], (L, HEAD_DIM), 0.05),
        "sink_a": nrm(ks[11], (L, HA), 1.0),
        "rel_bias_b": nrm(ks[12], (L, HB, 2 * NA_KH - 1, 2 * NA_KW - 1), 0.5),
        "w_proj_a": nrm(ks[13], (L, QA_W, D), QA_W ** -0.5),
        "w_proj_b": nrm(ks[14], (L, QB_W, D), QB_W ** -0.5),
        "w_out": nrm(ks[15], (L, D, D), D ** -0.5),
        "w_up": nrm(ks[16], (L, D, 2 * D_FF), D ** -0.5),
        "conv_w": nrm(ks[17], (L, CONV_W, 2 * D_FF), CONV_W ** -0.5),
        "conv_b": nrm(ks[18], (L, 2 * D_FF), 0.02),
        "w_down": nrm(ks[19], (L, D_FF, D), D_FF ** -0.5),
    }


def reference(x, c, ada_w, ada_b, norm_mix, norm_ffn, w_in, qn_a, kn_a, qn_b, kn_b,
              sink_a, rel_bias_b, w_proj_a, w_proj_b, w_out, w_up, conv_w, conv_b, w_down):
    B, S, _ = x.shape
    pos = jnp.arange(S, dtype=jnp.float32)
    c_act = jax.nn.silu(c)
    for l in range(DEPTH):
        mod = c_act @ ada_w[l] + ada_b[l]
        sh_a, sc_a, gt_a, sh_m, sc_m, gt_m = [t[:, None, :] for t in jnp.split(mod, 6, axis=-1)]

        h = rmsnorm(x, norm_mix[l]) * (1.0 + sc_a) + sh_a
        qa, ka, va, qb, kb, vb, ga, gb = split_cols(h @ w_in[l])
        qa = rotary_partial(rmsnorm(qa.reshape(B, S, HA, HEAD_DIM), qn_a[l]), pos)
        ka = rotary_partial(rmsnorm(ka.reshape(B, S, GA, HEAD_DIM), kn_a[l]), pos)
        va = va.reshape(B, S, GA, HEAD_DIM)
        ya = window_attention(qa, ka, va, sink_a[l]) @ w_proj_a[l]
        qb = rmsnorm(qb.reshape(B, S, HB, HEAD_DIM), qn_b[l])
        kb = rmsnorm(kb.reshape(B, S, HB, HEAD_DIM), kn_b[l])
        vb = vb.reshape(B, S, HB, HEAD_DIM)
        yb = neighbourhood_attention(qb, kb, vb, rel_bias_b[l]) @ w_proj_b[l]
        merged = jax.nn.sigmoid(ga) * ya + jax.nn.sigmoid(gb) * yb
        x = x + gt_a * (merged @ w_out[l])

        h = rmsnorm(x, norm_ffn[l]) * (1.0 + sc_m) + sh_m
        u = depthwise_conv(h @ w_up[l], conv_w[l], conv_b[l])
        g, v = jnp.split(u, 2, axis=-1)
        x = x + gt_m * ((jax.nn.silu(g) * v) @ w_down[l])
    return x
```

```python
import numpy as np
from contextlib import ExitStack
import concourse.bass as bass
import concourse.mybir as mybir
from concourse.bass_utils import run_bass_kernel_spmd

F32 = mybir.dt.float32
BF16 = mybir.dt.bfloat16
AF = mybir.ActivationFunctionType
ALU = mybir.AluOpType

P = 128
D = 2048
DC = 16
SEQ = 4096
NL = 2
HA, GA, HB = 8, 2, 8
INC = 8704
DFF = 5632
FC = 44
TKV = [2688, 2368]
TQ = [2369, 2049]
TF = [2368, 2048]
NEG = -1e30
EPS = 1e-6
QA0, KA0, VA0, QB0, KB0, VB0, GA0, GB0 = 0, 1024, 1280, 1536, 2560, 3584, 4608, 6656
SM_C = 0
SM_L = 16
SM_LSZ = 16 + 16 + 4 + 8 + 264 + 88
O_NM, O_NF, O_QNA, O_KNA, O_QNB, O_KNB, O_SINK, O_CW, O_CB = 0, 16, 32, 33, 34, 35, 36, 44, 308
SM_N = SM_L + NL * SM_LSZ
SEM_LIMIT = 30000


def tiles(lo, hi, step):
    t = [(s, min(step, hi - s)) for s in range(lo, hi, step)]
    if len(t) > 1 and t[-1][1] < 128:
        s0 = t[-2][0]
        tot = t[-2][1] + t[-1][1]
        a = (tot + 1) // 2
        t[-2:] = [(s0, a), (s0 + a, tot - a)]
    return t


class Tok:
    __slots__ = ("sid", "sem", "val")

    def __init__(self, sid, sem, val):
        self.sid, self.sem, self.val = sid, sem, val


class Buf:
    __slots__ = ("w", "r", "name")

    def __init__(self, name=""):
        self.w = None
        self.r = {}
        self.name = name


class Tile:
    __slots__ = ("ap", "buf")

    def __init__(self, ap, buf=None, name=""):
        self.ap = ap
        self.buf = buf if buf is not None else Buf(name)


class Eng:
    def __init__(self, prog, name):
        self.prog = prog
        self.name = name
        self.q = []
        self.waited = {}
        self.sem = None
        self.cnt = 0
        self.last = None
        self.ring = []
        self.rk = 0

    def wait(self, tok):
        if tok is None:
            return
        if self.waited.get(tok.sid, 0) >= tok.val:
            return
        self.waited[tok.sid] = tok.val
        self.q.append(("w", tok.sem, tok.val))

    def next_tok(self):
        if self.sem is None or self.cnt >= SEM_LIMIT:
            self.sem = self.prog.new_sem(self.name)
            self.cnt = 0
        self.cnt += 1
        self.last = Tok(self.sem[0], self.sem[1], self.cnt)
        return self.last


class Prog:
    def __init__(self, nc, es):
        self.nc = nc
        self.es = es
        self.nsem = 0
        self.E = {n: Eng(self, n) for n in ("pe", "act", "dve", "pool", "sp")}
        for qn, R in (("sp", 12), ("pool", 12), ("act", 8)):
            e = self.E[qn]
            e.ring = [[self.new_sem(qn + "r"), 0, None] for _ in range(R)]

    def new_sem(self, name):
        self.nsem += 1
        s = self.es.enter_context(self.nc.semaphore(f"{name}{self.nsem}"))
        return (self.nsem, s)

    def _deps(self, e, reads, writes):
        for b in reads:
            e.wait(b.buf.w)
        for b in writes:
            e.wait(b.buf.w)
            for t in b.buf.r.values():
                e.wait(t)

    def _commit(self, tok, reads, writes):
        for b in writes:
            b.buf.w = tok
            b.buf.r = {}
        for b in reads:
            o = b.buf.r.get(tok.sid)
            if o is None or o.val < tok.val:
                b.buf.r[tok.sid] = tok

    def op(self, en, fn, reads=(), writes=()):
        e = self.E[en]
        self._deps(e, reads, writes)
        tok = e.next_tok()
        e.q.append(("o", fn, tok.sem, 1))
        self._commit(tok, reads, writes)
        return tok

    def dma(self, qn, out_ap, in_ap, reads=(), writes=()):
        e = self.E[qn]
        self._deps(e, reads, writes)
        slot = e.ring[e.rk % len(e.ring)]
        e.rk += 1
        e.wait(slot[2])
        slot[1] += 16
        tok = Tok(slot[0][0], slot[0][1], slot[1])
        slot[2] = tok
        e.q.append(("o", (lambda g, o=out_ap, i=in_ap: g.dma_start(out=o, in_=i)), tok.sem, 16))
        self._commit(tok, reads, writes)
        return tok

    def barrier(self):
        toks = []
        for e in self.E.values():
            if e.last is not None:
                toks.append(e.last)
            for s in e.ring:
                if s[2] is not None:
                    toks.append(s[2])
        for e in self.E.values():
            for t in toks:
                e.wait(t)

    def emit(self):
        nc = self.nc
        E = self.E

        def run(e, g):
            for it in e.q:
                if it[0] == "w":
                    g.wait_ge(it[1], it[2])
                else:
                    ins = it[1](g)
                    ins.then_inc(it[2], it[3])

        with nc.Block() as block:
            @block.tensor
            def _(g):
                run(E["pe"], g)

            @block.scalar
            def _(g):
                run(E["act"], g)

            @block.vector
            def _(g):
                run(E["dve"], g)

            @block.gpsimd
            def _(g):
                run(E["pool"], g)

            @block.sync
            def _(g):
                run(E["sp"], g)


class Arena:
    def __init__(self, h32, nbytes):
        self.h32 = h32
        self.h16 = h32.bitcast(BF16)
        self.n = nbytes
        self.top = 0
        self.w32 = nbytes // 4
        self.hist = []

    def mark(self):
        return self.top

    def reset(self, m):
        self.top = m

    def alloc(self, shape, dt, name=""):
        free = 1
        for s in shape[1:]:
            free *= s
        esz = 4 if dt == F32 else 2
        nb = (free * esz + 31) // 32 * 32
        off = self.top
        self.top += nb
        assert self.top <= self.n, f"arena overflow {name} {self.top} > {self.n}"
        h = self.h32 if dt == F32 else self.h16
        o = off // esz
        ap = h[0:shape[0], o:o + free]
        if len(shape) == 3:
            ap = ap.rearrange("p (a b) -> p a b", a=shape[1])
        elif len(shape) == 4:
            ap = ap.rearrange("p (a b c) -> p a b c", a=shape[1], b=shape[2])
        t = Tile(ap, name=name)
        end = off + nb
        keep = []
        for (o0, o1, ob) in self.hist:
            if o0 < end and off < o1:
                toks = list(ob.r.values())
                if ob.w is not None:
                    toks.append(ob.w)
                for tk in toks:
                    cur = t.buf.r.get(tk.sid)
                    if cur is None or cur.val < tk.val:
                        t.buf.r[tk.sid] = tk
                if o0 >= off and o1 <= end:
                    continue
            keep.append((o0, o1, ob))
        keep.append((off, end, t.buf))
        self.hist = keep
        return t

    def bcast_mid(self, t, nrep, n):
        a = t.ap
        return bass.AP(a.tensor, a.offset, [[self.w32, a.shape[0]], [0, nrep], [1, n]])


class Ring:
    def __init__(self, items):
        self.items = items
        self.k = 0

    def next(self):
        t = self.items[self.k % len(self.items)]
        self.k += 1
        return t


def build_program(upto="all", dump=()):
    nc = bass.Bass("TRN2", target_bir_lowering=False)
    es = ExitStack()

    def din(name, shape, dt=F32):
        return nc.dram_tensor(name, list(shape), dt, kind="ExternalInput").ap()

    def dscr(name, shape, dt):
        kind = "ExternalOutput" if name in dump else "Internal"
        return nc.dram_tensor(name, list(shape), dt, kind=kind).ap()

    xin = din("xin", [DC, P, TKV[0]])
    sm_d = din("sm", [P, SM_N])
    ada_w = din("ada_w", [NL, D, 6 * D])
    ada_b = din("ada_b", [NL, 6 * D])
    w_in = din("w_in", [NL, D, INC])
    w_pa = din("w_proj_a", [NL, 1024, D])
    w_pb = din("w_proj_b", [NL, 1024, D])
    w_out = din("w_out", [NL, D, D])
    w_up = din("w_up", [NL, D, 2 * DFF])
    w_down = din("w_down", [NL, DFF, D])
    cs_d = din("cs", [32, 2, TKV[0]])
    wm_d = din("wmask", [P, 2, 3, P])
    nb_d = din("nbias", [NL, HB, P, 3, 5, P])
    rm_d = din("rmat", [32, 32])
    out_d = nc.dram_tensor("out", [DC, P, 2048], F32, kind="ExternalOutput").ap()

    xmid = [dscr(f"xmid{l}", [DC, P, TQ[l]], F32) for l in range(NL)]
    x1 = dscr("x1", [DC, P, TF[0]], F32)
    xsrc = [xin, x1]
    xdst = [x1, out_d]
    NKB = [(TKV[l] + 127) // 128 for l in range(NL)]
    qs = [dscr(f"qs{l}", [16, P, TQ[l]], BF16) for l in range(NL)]
    ks = [dscr(f"ks{l}", [10, P, TKV[l]], BF16) for l in range(NL)]
    vs = [dscr(f"vs{l}", [10, P, NKB[l], P], BF16) for l in range(NL)]
    gs = [dscr(f"gs{l}", [32, P, TQ[l]], BF16) for l in range(NL)]
    os_ = [dscr(f"os{l}", [16, P, TQ[l]], BF16) for l in range(NL)]
    a_s = None

    pg = Prog(nc, es)
    ARENA_BYTES = 206 * 1024
    h32 = es.enter_context(nc.sbuf_tensor("arena", [P, ARENA_BYTES // 4], F32))
    ar = Arena(h32, ARENA_BYTES)
    psum = es.enter_context(nc.psum_tensor("psum", [P, 4096], F32))
    banks = [Tile(psum[:, i * 512:(i + 1) * 512], name=f"bank{i}") for i in range(8)]

    sm = ar.alloc([P, SM_N], F32, "sm")
    modfm = ar.alloc([P, NL, 96], F32, "modfm")
    der = ar.alloc([P, NL, 6, 16], F32, "der")
    ones_bf = ar.alloc([P, P], BF16, "ones_bf")
    ones_f = ar.alloc([P, P], F32, "ones_f")
    rmat = ar.alloc([32, 32], F32, "rmat")
    wmask = ar.alloc([P, 2, 3, P], F32, "wmask")
    cact = ar.alloc([P, 16], BF16, "cact")
    qsc = ar.alloc([P, NL, 2], F32, "qsc")
    esink = ar.alloc([P, NL, 8], F32, "esink")
    base_mark = ar.mark()

    def smc(l, off, n=1):
        c0 = SM_L + l * SM_LSZ + off
        return sm.ap[:, c0:c0 + n]

    pg.dma("sp", sm.ap, sm_d, writes=[sm])
    pg.dma("sp", rmat.ap, rm_d, writes=[rmat])
    pg.dma("sp", wmask.ap, wm_d, writes=[wmask])
    pg.op("dve", lambda g: g.memset(ones_bf.ap, 1.0), writes=[ones_bf])
    pg.op("dve", lambda g: g.memset(ones_f.ap, 1.0), writes=[ones_f])
    pg.op("act", lambda g: g.activation(out=cact.ap, in_=sm.ap[:, SM_C:SM_C + 16], func=AF.Silu),
          reads=[sm], writes=[cact])
    for l in range(NL):
        pg.op("dve", lambda g, l=l: g.tensor_scalar(qsc.ap[:, l, 0:1], smc(l, O_QNA), float(128 ** -0.5), None, ALU.mult),
              reads=[sm], writes=[qsc])
        pg.op("dve", lambda g, l=l: g.tensor_scalar(qsc.ap[:, l, 1:2], smc(l, O_QNB), float(128 ** -0.5), None, ALU.mult),
              reads=[sm], writes=[qsc])
        pg.op("act", lambda g, l=l: g.activation(out=esink.ap[:, l, :], in_=smc(l, O_SINK, 8), func=AF.Exp),
              reads=[sm], writes=[esink])

    def ada_layer(l, wr, br, rr, pr, p2):
        adaw = {}

        def ada_load(g_):
            if g_ >= 24:
                return
            wt_ = wr.next()
            pg.dma("pool", wt_.ap, ada_w[l, :, g_ * 512:(g_ + 1) * 512].rearrange("(k p) n -> p k n", p=P), writes=[wt_])
            adaw[g_] = wt_
        ada_load(0)
        for g in range(24):
            ada_load(g + 1)
            wt = adaw.pop(g)
            bt = br.next()
            c0 = g * 512
            pg.dma("sp", bt.ap, ada_b[l:l + 1, c0:c0 + 512], writes=[bt])
            pa = pr.next()

            def f(g_, wt=wt, pa=pa):
                for k in range(DC):
                    ins = g_.matmul(pa.ap[0:1, :], cact.ap[:, k:k + 1], wt.ap[:, k, :], start=(k == 0), stop=(k == DC - 1))
                return ins
            pg.op("pe", f, reads=[wt, cact], writes=[pa])
            row = rr.next()
            pg.op("dve", lambda g_, row=row, pa=pa, bt=bt: g_.tensor_tensor(row.ap, pa.ap[0:1, :], bt.ap, ALU.add),
                  reads=[pa, bt], writes=[row])
            yield
            pb = p2.next()

            def f2(g_, row=row, pb=pb):
                for s_ in range(4):
                    ins = g_.matmul(pb.ap[:, s_:s_ + 1], row.ap[0:1, s_ * P:(s_ + 1) * P], ones_f.ap[0:1, 0:1], start=True, stop=True)
                return ins
            pg.op("pe", f2, reads=[row, ones_f], writes=[pb])
            pg.op("act", lambda g_, pb=pb, g=g: g_.copy(modfm.ap[:, l, g * 4:(g + 1) * 4], pb.ap[:, 0:4]),
                  reads=[pb], writes=[modfm])
        for s_, (isc, ish, igt, ogam) in enumerate(((16, 0, 32, O_NM), (64, 48, 80, O_NF))):
            pg.op("dve", lambda g_, s_=s_, isc=isc, ogam=ogam: g_.scalar_tensor_tensor(
                der.ap[:, l, 3 * s_ + 0, :], modfm.ap[:, l, isc:isc + 16], 1.0, smc(l, ogam, 16), ALU.add, ALU.mult),
                reads=[modfm, sm], writes=[der])
            pg.op("dve", lambda g_, s_=s_, ish=ish: g_.tensor_copy(der.ap[:, l, 3 * s_ + 1, :], modfm.ap[:, l, ish:ish + 16]),
                  reads=[modfm], writes=[der])
            pg.op("dve", lambda g_, s_=s_, igt=igt: g_.tensor_copy(der.ap[:, l, 3 * s_ + 2, :], modfm.ap[:, l, igt:igt + 16]),
                  reads=[modfm], writes=[der])

    def ada_rings():
        wr = Ring([ar.alloc([P, DC, 512], BF16, f"adaw{i}") for i in range(2)])
        br = Ring([ar.alloc([1, 512], F32, f"adab{i}") for i in range(2)])
        rr = Ring([ar.alloc([1, 512], F32, f"adar{i}") for i in range(3)])
        return wr, br, rr

    def stage_ada():
        m0 = ar.mark()
        wr, br, rr = ada_rings()
        for _ in ada_layer(0, wr, br, rr, Ring(banks[0:2]), Ring(banks[2:4])):
            pass
        pg.barrier()
        ar.reset(m0)

    def stage_norm(xd, lo, hi, hT, l, which, NT=512, depth=2):
        m0 = ar.mark()
        H = DC // 2
        xr = Ring([(ar.alloc([P, H, NT], F32, f"nxa{i}"), ar.alloc([P, H, NT], F32, f"nxb{i}")) for i in range(depth)])
        sq = Ring([ar.alloc([P, DC, NT], BF16, f"nsq{i}") for i in range(depth)])
        rs = Ring([ar.alloc([P, NT], F32, f"nrs{i}") for i in range(depth)])
        pr = Ring(banks[0:3])
        A = der.ap[:, l, 3 * which + 0, :]
        B = der.ap[:, l, 3 * which + 1, :]
        def ntile(s, n):
            xa, xb = xr.next()
            pg.dma("sp", xa.ap[:, :, 0:n], xd[0:H, :, s:s + n].rearrange("c p t -> p c t"), writes=[xa])
            pg.dma("pool", xb.ap[:, :, 0:n], xd[H:DC, :, s:s + n].rearrange("c p t -> p c t"), writes=[xb])
            st = sq.next()
            pg.op("act", lambda g: g.activation(out=st.ap[:, 0:H, 0:n], in_=xa.ap[:, :, 0:n], func=AF.Square),
                  reads=[xa], writes=[st])
            pg.op("act", lambda g: g.activation(out=st.ap[:, H:DC, 0:n], in_=xb.ap[:, :, 0:n], func=AF.Square),
                  reads=[xb], writes=[st])
            pa = pr.next()

            def f(g):
                for k in range(DC):
                    ins = g.matmul(pa.ap[:, 0:n], ones_bf.ap, st.ap[:, k, 0:n], start=(k == 0), stop=(k == DC - 1))
                return ins
            pg.op("pe", f, reads=[st, ones_bf], writes=[pa])
            yield
            r = rs.next()
            pg.op("act", lambda g: g.activation(out=r.ap[:, 0:n], in_=pa.ap[:, 0:n], func=AF.Ln, bias=EPS, scale=1.0 / D),
                  reads=[pa], writes=[r])
            pg.op("act", lambda g: g.activation(out=r.ap[:, 0:n], in_=r.ap[:, 0:n], func=AF.Exp, scale=-0.5), reads=[r], writes=[r])
            for xh in (xa, xb):
                pg.op("dve", lambda g, xh=xh: g.tensor_tensor(xh.ap[:, :, 0:n], xh.ap[:, :, 0:n], ar.bcast_mid(Tile(r.ap[:, 0:n]), H, n), ALU.mult),
                      reads=[xh, r], writes=[xh])

            def f3(g):
                for k in range(DC):
                    src = xa if k < H else xb
                    ins = g.activation(out=hT.ap[:, k, s - lo:s - lo + n], in_=src.ap[:, k % H, 0:n], func=AF.Identity,
                                       bias=B[:, k:k + 1], scale=A[:, k:k + 1])
                return ins
            pg.op("act", f3, reads=[xa, xb, der], writes=[hT])

        prev = None
        for (s_, n_) in tiles(lo, hi, NT):
            cur = ntile(s_, n_)
            next(cur)
            if prev is not None:
                for _ in prev:
                    pass
            prev = cur
        for _ in prev:
            pass
        ar.reset(m0)

    def stage_b1(l):
        m0 = ar.mark()
        tkv, tq = TKV[l], TQ[l]
        hT = ar.alloc([P, DC, tkv], BF16, "hT")
        stage_norm(xsrc[l], 0, tkv, hT, l, 0)
        cs = ar.alloc([32, 2, TKV[0]], F32, "cs")
        pg.dma("sp", cs.ap, cs_d, writes=[cs])
        wr = Ring([ar.alloc([P, DC, 512], BF16, f"b1w{i}") for i in range(2)])
        sqr = Ring([ar.alloc([P, 512], BF16, f"b1sq{i}") for i in range(3)])
        rsr = Ring([ar.alloc([P, 512], F32, f"b1rs{i}") for i in range(3)])
        obr = Ring([ar.alloc([P, 512], BF16, f"b1ob{i}") for i in range(6)])
        t32r = Ring([ar.alloc([32, 512], F32, f"b1t{i}") for i in range(4)])
        o1r = Ring([ar.alloc([32, 512], F32, f"b1o1{i}") for i in range(3)])
        o2r = Ring([ar.alloc([32, 512], F32, f"b1o2{i}") for i in range(3)])
        pmain = Ring(banks[0:4])
        pss = Ring(banks[4:6])
        prot = Ring(banks[6:8])

        def mm_fm(wt, ci, s, n):
            pa = pmain.next()

            def f(g, pa=pa):
                for k in range(DC):
                    ins = g.matmul(pa.ap[:, 0:n], wt.ap[:, k, ci * P:(ci + 1) * P], hT.ap[:, k, s:s + n],
                                   start=(k == 0), stop=(k == DC - 1))
                return ins
            pg.op("pe", f, reads=[wt, hT], writes=[pa])
            return pa

        pending = []

        def pump():
            for g_ in list(pending):
                try:
                    next(g_)
                except StopIteration:
                    pending.remove(g_)

        def launch(gen):
            pump()
            try:
                next(gen)
                pending.append(gen)
            except StopIteration:
                pass

        def epi_qk(pa, s, n, gain, rot, dst):
            sq = sqr.next()
            pg.op("act", lambda g: g.activation(out=sq.ap[:, 0:n], in_=pa.ap[:, 0:n], func=AF.Square), reads=[pa], writes=[sq])
            yield
            ps = pss.next()
            pg.op("pe", lambda g: g.matmul(ps.ap[:, 0:n], ones_bf.ap, sq.ap[:, 0:n], start=True, stop=True),
                  reads=[sq, ones_bf], writes=[ps])
            r = rsr.next()
            pg.op("act", lambda g: g.activation(out=r.ap[:, 0:n], in_=ps.ap[:, 0:n], func=AF.Ln, bias=EPS, scale=1.0 / 128),
                  reads=[ps], writes=[r])
            pg.op("act", lambda g: g.activation(out=r.ap[:, 0:n], in_=r.ap[:, 0:n], func=AF.Exp, scale=-0.5),
                  reads=[r], writes=[r])
            ob = obr.next()
            pg.op("dve", lambda g: g.scalar_tensor_tensor(ob.ap[:, 0:n], pa.ap[:, 0:n], gain, r.ap[:, 0:n], ALU.mult, ALU.mult),
                  reads=[pa, r, sm, qsc], writes=[ob])
            if rot:
                t32 = t32r.next()
                pg.op("dve", lambda g: g.scalar_tensor_tensor(t32.ap[:, 0:n], pa.ap[0:32, 0:n], gain[0:32, :], r.ap[0:32, 0:n], ALU.mult, ALU.mult),
                      reads=[pa, r, sm, qsc], writes=[t32])
                yield
                yield
                pr = prot.next()
                pg.op("pe", lambda g: g.matmul(pr.ap[0:32, 0:n], rmat.ap, t32.ap[:, 0:n], start=True, stop=True),
                      reads=[t32, rmat], writes=[pr])
                o1 = o1r.next()
                pg.op("pool", lambda g: g.tensor_tensor(o1.ap[:, 0:n], t32.ap[:, 0:n], cs.ap[:, 0, s:s + n], ALU.mult),
                      reads=[t32, cs], writes=[o1])
                o2 = o2r.next()
                pg.op("dve", lambda g: g.tensor_tensor(o2.ap[:, 0:n], pr.ap[0:32, 0:n], cs.ap[:, 1, s:s + n], ALU.mult),
                      reads=[pr, cs], writes=[o2])
                pg.op("dve", lambda g: g.tensor_tensor(ob.ap[0:32, 0:n], o1.ap[:, 0:n], o2.ap[:, 0:n], ALU.add),
                      reads=[o1, o2], writes=[ob])
            pg.dma("sp", dst[:, s:s + n], ob.ap[:, 0:n], reads=[ob])

        def epi_gate(pa, s, n, dst):
            ob = obr.next()
            pg.op("act", lambda g: g.activation(out=ob.ap[:, 0:n], in_=pa.ap[:, 0:n], func=AF.Sigmoid), reads=[pa], writes=[ob])
            pg.dma("sp", dst[:, s:s + n], ob.ap[:, 0:n], reads=[ob])

        def do_v(wt, c0, ncols, h0):
            nh = ncols // P
            for tb in range(NKB[l]):
                t0 = tb * P
                nt = min(P, tkv - t0)
                pa = pmain.next()

                def f(g, pa=pa, t0=t0, nt=nt):
                    for k in range(DC):
                        ins = g.matmul(pa.ap[0:nt, 0:ncols], hT.ap[:, k, t0:t0 + nt], wt.ap[:, k, c0:c0 + ncols],
                                       start=(k == 0), stop=(k == DC - 1))
                    return ins
                pg.op("pe", f, reads=[wt, hT], writes=[pa])
                pump()
                ob = obr.next()
                pg.op("act", lambda g, ob=ob, pa=pa, nt=nt: g.copy(ob.ap[0:nt, 0:ncols], pa.ap[0:nt, 0:ncols]), reads=[pa], writes=[ob])
                pg.dma("sp", vs[l][h0:h0 + nh, 0:nt, tb, :].rearrange("h k d -> k h d"),
                       ob.ap[0:nt, 0:ncols].rearrange("k (h d) -> k h d", h=nh), reads=[ob])

        qnA, knA = qsc.ap[:, l, 0:1], smc(l, O_KNA)
        qnB, knB = qsc.ap[:, l, 1:2], smc(l, O_KNB)
        b1w = {}

        def b1_load(grp_):
            if grp_ >= 17:
                return
            wt_ = wr.next()
            pg.dma("pool", wt_.ap, w_in[l, :, grp_ * 512:(grp_ + 1) * 512].rearrange("(k p) n -> p k n", p=P), writes=[wt_])
            b1w[grp_] = wt_
        b1_load(0)
        for grp in range(17):
            b1_load(grp + 1)
            wt = b1w.pop(grp)
            c0 = grp * 512
            for ci in range(4):
                col = c0 + ci * P
                if col < KA0:
                    h = col // P
                    for (s, n) in tiles(0, tq, 512):
                        launch(epi_qk(mm_fm(wt, ci, s, n), s, n, qnA, True, qs[l][h]))
                elif col < VA0:
                    gidx = (col - KA0) // P
                    for (s, n) in tiles(0, tkv, 512):
                        launch(epi_qk(mm_fm(wt, ci, s, n), s, n, knA, True, ks[l][gidx]))
                elif col < QB0:
                    if col == VA0:
                        do_v(wt, ci * P, 256, 0)
                elif col < KB0:
                    h = (col - QB0) // P
                    for (s, n) in tiles(0, tq, 512):
                        launch(epi_qk(mm_fm(wt, ci, s, n), s, n, qnB, False, qs[l][8 + h]))
                elif col < VB0:
                    h = (col - KB0) // P
                    for (s, n) in tiles(0, tkv, 512):
                        launch(epi_qk(mm_fm(wt, ci, s, n), s, n, knB, False, ks[l][2 + h]))
                elif col < GA0:
                    if ci == 0:
                        do_v(wt, 0, 512, 2 + (col - VB0) // P)
                else:
                    gi = (col - GA0) // P
                    for (s, n) in tiles(0, tq, 512):
                        epi_gate(mm_fm(wt, ci, s, n), s, n, gs[l][gi])
                        pump()
        while pending:
            pump()
        pg.barrier()
        ar.reset(m0)

    def stage_b2(l):
        m0 = ar.mark()
        tkv, tq, nkb_all = TKV[l], TQ[l], NKB[l]
        kpad = nkb_all * P
        qr = Ring([ar.alloc([P, tq], BF16, f"q{i}") for i in range(2)])
        kr = Ring([ar.alloc([P, kpad], BF16, f"k{i}") for i in range(2)])
        vr = Ring([ar.alloc([P, nkb_all, P], BF16, f"v{i}") for i in range(2)])
        orr = Ring([ar.alloc([P, tq], BF16, f"o{i}") for i in range(2)])
        nbr = Ring([ar.alloc([P, 3, 5, P], F32, f"nb{i}") for i in range(2)])
        tr = Ring([ar.alloc([P, 5, P], F32, f"t{i}") for i in range(4)])
        prr = Ring([ar.alloc([P, 5, P], BF16, f"p{i}") for i in range(4)])
        rir = Ring([ar.alloc([P, P], F32, f"ri{i}") for i in range(4)])
        adagen = None
        if l == 0:
            depth = 2
            ps_s = Ring([Tile(psum[:, k * 1024:(k + 1) * 1024], name=f"S{k}") for k in range(2)])
            ps_o = Ring(banks[4:7])
            awr, abr, arr = ada_rings()
            adagen = ada_layer(1, awr, abr, arr, Ring([banks[7]]), Ring([banks[7]]))
        else:
            depth = 3
            ps_s = Ring([Tile(psum[:, k * 1024:(k + 1) * 1024], name=f"S{k}") for k in range(3)])
            ps_o = Ring(banks[6:8])

        def ada_step(k):
            if adagen is not None:
                for _ in range(k):
                    next(adagen, None)
        if kpad > tkv:
            for t in kr.items:
                pg.op("dve", lambda g, t=t: g.memset(t.ap[:, tkv:kpad], 0.0), writes=[t])
            rem = tkv - (nkb_all - 1) * P
            for t in vr.items:
                pg.op("dve", lambda g, t=t, rem=rem: g.memset(t.ap[rem:P, nkb_all - 1, :], 0.0), writes=[t])

        def load_kv(idx):
            kt, vt = kr.next(), vr.next()
            pg.dma("sp", kt.ap[:, 0:tkv], ks[l][idx], writes=[kt])
            full = tkv // P
            pg.dma("sp", vt.ap[:, 0:full, :], vs[l][idx, :, 0:full, :], writes=[vt])
            if full < nkb_all:
                rem = tkv - full * P
                pg.dma("sp", vt.ap[0:rem, full, :], vs[l][idx, 0:rem, full, :], writes=[vt])
            return kt, vt

        def head(hq, kt, vt, nkb, tab, is_win):
            qt = qr.next()
            pg.dma("sp", qt.ap, qs[l][hq], writes=[qt])
            ot = orr.next()
            nqb = (tq + P - 1) // P
            def qblock(i):
                q0 = i * P
                nq = min(P, tq - q0)
                if is_win:
                    ty, kb0 = min(i, 1), max(i - 1, 0)
                else:
                    ty, kb0 = min(i, 2), max(i - 2, 0)
                S = ps_s.next()
                S3 = S.ap[:, 0:nkb * P].rearrange("p (j q) -> p j q", j=nkb)

                def f(g):
                    for j in range(nkb):
                        ins = g.matmul(S3[:, j, 0:nq], kt.ap[:, (kb0 + j) * P:(kb0 + j + 1) * P], qt.ap[:, q0:q0 + nq],
                                       start=True, stop=True)
                    return ins
                pg.op("pe", f, reads=[kt, qt], writes=[S])
                T_ = tr.next()
                pg.op("dve", lambda g: g.tensor_tensor(T_.ap[:, 0:nkb, 0:nq], S3[:, :, 0:nq], tab.ap[:, ty, 0:nkb, 0:nq], ALU.add),
                      reads=[S, tab], writes=[T_])
                Pm = prr.next()
                pg.op("act", lambda g: g.activation(out=Pm.ap[:, 0:nkb, 0:nq], in_=T_.ap[:, 0:nkb, 0:nq], func=AF.Exp),
                      reads=[T_], writes=[Pm])
                yield
                O = ps_o.next()

                def f2(g):
                    for j in range(nkb):
                        g.matmul(O.ap[:, 0:nq], vt.ap[:, kb0 + j, :], Pm.ap[:, j, 0:nq], start=(j == 0), stop=(j == nkb - 1))
                    for j in range(nkb):
                        ins = g.matmul(O.ap[:, P:P + nq], ones_bf.ap, Pm.ap[:, j, 0:nq], start=(j == 0), stop=(j == nkb - 1))
                    return ins
                pg.op("pe", f2, reads=[vt, Pm, ones_bf], writes=[O])
                ri = rir.next()
                bias_ = esink.ap[:, l, hq:hq + 1] if is_win else 0.0
                pg.op("act", lambda g: g.activation(out=ri.ap[:, 0:nq], in_=O.ap[:, P:P + nq], func=AF.Ln, bias=bias_),
                      reads=[O, esink], writes=[ri])
                pg.op("act", lambda g: g.activation(out=ri.ap[:, 0:nq], in_=ri.ap[:, 0:nq], func=AF.Exp, scale=-1.0),
                      reads=[ri], writes=[ri])
                pg.op("dve", lambda g: g.tensor_tensor(ot.ap[:, q0:q0 + nq], O.ap[:, 0:nq], ri.ap[:, 0:nq], ALU.mult),
                      reads=[O, ri], writes=[ot])

            pend = []
            for i in range(nqb):
                cur = qblock(i)
                next(cur)
                pend.append(cur)
                if len(pend) >= depth:
                    for _ in pend.pop(0):
                        pass
            for g_ in pend:
                for _ in g_:
                    pass
            return ot

        for gi in range(GA):
            kt, vt = load_kv(gi)
            for r in range(HA // GA):
                hq = gi * (HA // GA) + r
                ot = head(hq, kt, vt, 3, wmask, True)
                pg.dma("pool", os_[l][hq], ot.ap, reads=[ot])
                ada_step(1)
        for hb in range(HB):
            kt, vt = load_kv(2 + hb)
            tab = nbr.next()
            pg.dma("sp", tab.ap, nb_d[l, hb], writes=[tab])
            ot = head(8 + hb, kt, vt, 5, tab, False)
            pg.dma("pool", os_[l][8 + hb], ot.ap, reads=[ot])
            ada_step(2)
        ada_step(100)
        pg.barrier()
        ar.reset(m0)

    def stage_c(l):
        tq = TQ[l]
        half = (tq // 2 + 127) // 128 * 128
        def group_c(t0, ng):
            m0 = ar.mark()
            O = ar.alloc([P, 16, ng], BF16, "O")
            mg = ar.alloc([P, DC, ng], BF16, "mg")
            pg.dma("sp", O.ap, os_[l][:, :, t0:t0 + ng].rearrange("h p t -> p h t"), writes=[O])
            m1 = ar.mark()
            wa = Ring([ar.alloc([P, 8, 512], BF16, f"wa{i}") for i in range(2)])
            wb = Ring([ar.alloc([P, 8, 512], BF16, f"wb{i}") for i in range(2)])
            sga = Ring([ar.alloc([P, ng], BF16, f"sga{i}") for i in range(2)])
            sgb = Ring([ar.alloc([P, ng], BF16, f"sgb{i}") for i in range(2)])
            m1r = Ring([ar.alloc([P, 512], F32, f"m1{i}") for i in range(2)])
            m2r = Ring([ar.alloc([P, 512], F32, f"m2{i}") for i in range(2)])
            pA = Ring(banks[0:4])
            pB = Ring(banks[4:8])
            c1w = {}

            def c1_load(cg_):
                if cg_ >= 4:
                    return
                a_, b_ = wa.next(), wb.next()
                pg.dma("pool", a_.ap, w_pa[l, :, cg_ * 512:(cg_ + 1) * 512].rearrange("(k p) n -> p k n", p=P), writes=[a_])
                pg.dma("pool", b_.ap, w_pb[l, :, cg_ * 512:(cg_ + 1) * 512].rearrange("(k p) n -> p k n", p=P), writes=[b_])
                c1w[cg_] = (a_, b_)
            c1_load(0)
            for cg in range(4):
                c1_load(cg + 1)
                wta, wtb = c1w.pop(cg)
                c0 = cg * 512
                for ci in range(4):
                    j = cg * 4 + ci
                    ga_t, gb_t = sga.next(), sgb.next()
                    pg.dma("sp", ga_t.ap, gs[l][j, :, t0:t0 + ng], writes=[ga_t])
                    pg.dma("sp", gb_t.ap, gs[l][16 + j, :, t0:t0 + ng], writes=[gb_t])
                    for (s, n) in tiles(0, ng, 512):
                        pa, pb = pA.next(), pB.next()

                        def f(g, pa=pa, pb=pb, wta=wta, wtb=wtb, ci=ci, s=s, n=n):
                            for k in range(8):
                                g.matmul(pa.ap[:, 0:n], wta.ap[:, k, ci * P:(ci + 1) * P], O.ap[:, k, s:s + n], start=(k == 0), stop=(k == 7))
                            for k in range(8):
                                ins = g.matmul(pb.ap[:, 0:n], wtb.ap[:, k, ci * P:(ci + 1) * P], O.ap[:, 8 + k, s:s + n], start=(k == 0), stop=(k == 7))
                            return ins
                        pg.op("pe", f, reads=[wta, wtb, O], writes=[pa, pb])
                        t1, t2 = m1r.next(), m2r.next()
                        pg.op("dve", lambda g, t1=t1, pa=pa, ga_t=ga_t, s=s, n=n: g.tensor_tensor(t1.ap[:, 0:n], pa.ap[:, 0:n], ga_t.ap[:, s:s + n], ALU.mult),
                              reads=[pa, ga_t], writes=[t1])
                        pg.op("dve", lambda g, t2=t2, pb=pb, gb_t=gb_t, s=s, n=n: g.tensor_tensor(t2.ap[:, 0:n], pb.ap[:, 0:n], gb_t.ap[:, s:s + n], ALU.mult),
                              reads=[pb, gb_t], writes=[t2])
                        pg.op("dve", lambda g, t1=t1, t2=t2, j=j, s=s, n=n: g.tensor_tensor(mg.ap[:, j, s:s + n], t1.ap[:, 0:n], t2.ap[:, 0:n], ALU.add),
                              reads=[t1, t2], writes=[mg])
            ar.reset(m1)
            wo = Ring([ar.alloc([P, DC, 512], BF16, f"wo{i}") for i in range(2)])
            xr = Ring([ar.alloc([P, 512], F32, f"cx{i}") for i in range(3)])
            xo = Ring([ar.alloc([P, 512], F32, f"cxo{i}") for i in range(3)])
            pA = Ring(banks[0:8])
            G = der.ap[:, l, 2, :]
            c2w = {}

            def c2_load(cg_):
                if cg_ >= 4:
                    return
                w_ = wo.next()
                pg.dma("pool", w_.ap, w_out[l, :, cg_ * 512:(cg_ + 1) * 512].rearrange("(k p) n -> p k n", p=P), writes=[w_])
                c2w[cg_] = w_
            c2_load(0)
            for cg in range(4):
                c2_load(cg + 1)
                wt = c2w.pop(cg)
                c0 = cg * 512
                for ci in range(4):
                    j = cg * 4 + ci
                    for (s, n) in tiles(0, ng, 512):
                        xt = xr.next()
                        pg.dma("sp", xt.ap[:, 0:n], xsrc[l][j, :, t0 + s:t0 + s + n], writes=[xt])
                        pa = pA.next()

                        def f(g, pa=pa, wt=wt, ci=ci, s=s, n=n):
                            for k in range(DC):
                                ins = g.matmul(pa.ap[:, 0:n], wt.ap[:, k, ci * P:(ci + 1) * P], mg.ap[:, k, s:s + n], start=(k == 0), stop=(k == DC - 1))
                            return ins
                        pg.op("pe", f, reads=[wt, mg], writes=[pa])
                        xn = xo.next()
                        pg.op("dve", lambda g, xn=xn, pa=pa, xt=xt, j=j, n=n: g.scalar_tensor_tensor(xn.ap[:, 0:n], pa.ap[:, 0:n], G[:, j:j + 1], xt.ap[:, 0:n], ALU.mult, ALU.add),
                              reads=[pa, xt, der], writes=[xn])
                        pg.dma("act", xmid[l][j, :, t0 + s:t0 + s + n], xn.ap[:, 0:n], reads=[xn])
            ar.reset(m0)

        for (t0_, ng_) in tiles(0, tq, half):
            group_c(t0_, ng_)
        pg.barrier()

    def stage_d(l):
        tf = TF[l]
        G3 = (tf // 3 + 63) // 64 * 64
        Gm = der.ap[:, l, 5, :]
        def group_d(t0, ng):
            m0 = ar.mark()
            aT = ar.alloc([P, FC, ng], BF16, "aT")
            m1 = ar.mark()
            ulo, uhi = max(t0 - 1, 0), t0 + ng + 1
            nU = uhi - ulo
            off0 = 1 if t0 == 0 else 0
            h2 = ar.alloc([P, DC, nU], BF16, "h2")
            stage_norm(xmid[l], ulo, uhi, h2, l, 1, NT=256, depth=3)
            wg = Ring([ar.alloc([P, DC, 256], BF16, f"wg{i}") for i in range(2)])
            wv = Ring([ar.alloc([P, DC, 256], BF16, f"wv{i}") for i in range(2)])
            ugr = Ring([ar.alloc([P, ng + 2], F32, f"ug{i}") for i in range(2)])
            uvr = Ring([ar.alloc([P, ng + 2], F32, f"uv{i}") for i in range(2)])
            cgr = Ring([ar.alloc([P, ng], F32, f"cg{i}") for i in range(2)])
            cvr = Ring([ar.alloc([P, ng], F32, f"cv{i}") for i in range(2)])
            pA = Ring(banks[0:8])
            if off0:
                for t in ugr.items + uvr.items:
                    pg.op("dve", lambda g, t=t: g.memset(t.ap[:, 0:1], 0.0), writes=[t])
            d1w = {}

            def d1_load(i_):
                if i_ >= FC // 2:
                    return
                a_, b_ = wg.next(), wv.next()
                pg.dma("pool", a_.ap, w_up[l, :, i_ * 256:(i_ + 1) * 256].rearrange("(k p) n -> p k n", p=P), writes=[a_])
                pg.dma("pool", b_.ap, w_up[l, :, DFF + i_ * 256:DFF + (i_ + 1) * 256].rearrange("(k p) n -> p k n", p=P), writes=[b_])
                d1w[i_] = (a_, b_)
            d1_load(0)
            for mg_ in range(FC // 2):
                d1_load(mg_ + 1)
                wgt, wvt = d1w.pop(mg_)
                c0 = mg_ * 256
                for ci in range(2):
                    m = mg_ * 2 + ci
                    rows = []
                    for (wt, rr, mc) in ((wgt, ugr, m), (wvt, uvr, FC + m)):
                        row = rr.next()
                        for (s, n) in tiles(0, nU, 512):
                            pa = pA.next()

                            def f(g, pa=pa, wt=wt, s=s, n=n, ci=ci):
                                for k in range(DC):
                                    ins = g.matmul(pa.ap[:, 0:n], wt.ap[:, k, ci * P:(ci + 1) * P], h2.ap[:, k, s:s + n], start=(k == 0), stop=(k == DC - 1))
                                return ins
                            pg.op("pe", f, reads=[wt, h2], writes=[pa])
                            pg.op("act", lambda g, row=row, pa=pa, s=s, n=n: g.copy(row.ap[:, off0 + s:off0 + s + n], pa.ap[:, 0:n]),
                                  reads=[pa], writes=[row])
                        rows.append((row, mc))
                    outs = []
                    for (row, mc), cr in zip(rows, (cgr, cvr)):
                        c = cr.next()
                        w0 = smc(l, O_CW + 0 * 88 + mc)
                        w1 = smc(l, O_CW + 1 * 88 + mc)
                        w2 = smc(l, O_CW + 2 * 88 + mc)
                        cb = smc(l, O_CB + mc)
                        pg.op("dve", lambda g, c=c, row=row, w1=w1, cb=cb: g.tensor_scalar(c.ap, row.ap[:, 1:ng + 1], w1, cb, ALU.mult, ALU.add),
                              reads=[row, sm], writes=[c])
                        pg.op("dve", lambda g, c=c, row=row, w0=w0: g.scalar_tensor_tensor(c.ap, row.ap[:, 0:ng], w0, c.ap, ALU.mult, ALU.add),
                              reads=[row, sm, c], writes=[c])
                        pg.op("dve", lambda g, c=c, row=row, w2=w2: g.scalar_tensor_tensor(c.ap, row.ap[:, 2:ng + 2], w2, c.ap, ALU.mult, ALU.add),
                              reads=[row, sm, c], writes=[c])
                        outs.append(c)
                    cg_, cv_ = outs
                    pg.op("act", lambda g, cg_=cg_: g.activation(out=cg_.ap, in_=cg_.ap, func=AF.Silu), reads=[cg_], writes=[cg_])
                    pg.op("dve", lambda g, cg_=cg_, cv_=cv_, m=m: g.tensor_tensor(aT.ap[:, m, :], cg_.ap, cv_.ap, ALU.mult),
                          reads=[cg_, cv_], writes=[aT])
            ar.reset(m1)
            wd = Ring([ar.alloc([P, FC, 256], BF16, f"wd{i}") for i in range(2)])
            xr = Ring([ar.alloc([P, 512], F32, f"dx{i}") for i in range(3)])
            xo = Ring([ar.alloc([P, 512], F32, f"dxo{i}") for i in range(3)])
            pA = Ring(banks[0:8])
            d2w = {}

            def d2_load(cg_):
                if cg_ >= 8:
                    return
                w_ = wd.next()
                pg.dma("pool", w_.ap, w_down[l, :, cg_ * 256:(cg_ + 1) * 256].rearrange("(k p) n -> p k n", p=P), writes=[w_])
                d2w[cg_] = w_
            d2_load(0)
            for cg in range(8):
                d2_load(cg + 1)
                wt = d2w.pop(cg)
                c0 = cg * 256
                for ci in range(2):
                    j = cg * 2 + ci
                    for (s, n) in tiles(0, ng, 512):
                        xt = xr.next()
                        pg.dma("sp", xt.ap[:, 0:n], xmid[l][j, :, t0 + s:t0 + s + n], writes=[xt])
                        pa = pA.next()

                        def f(g, pa=pa, wt=wt, ci=ci, s=s, n=n):
                            for k in range(FC):
                                ins = g.matmul(pa.ap[:, 0:n], wt.ap[:, k, ci * P:(ci + 1) * P], aT.ap[:, k, s:s + n], start=(k == 0), stop=(k == FC - 1))
                            return ins
                        pg.op("pe", f, reads=[wt, aT], writes=[pa])
                        xn = xo.next()
                        pg.op("dve", lambda g, xn=xn, pa=pa, xt=xt, j=j, n=n: g.scalar_tensor_tensor(xn.ap[:, 0:n], pa.ap[:, 0:n], Gm[:, j:j + 1], xt.ap[:, 0:n], ALU.mult, ALU.add),
                              reads=[pa, xt, der], writes=[xn])
                        lim = min(t0 + s + n, xdst[l].shape[2])
                        if lim > t0 + s:
                            pg.dma("act", xdst[l][j, :, t0 + s:lim], xn.ap[:, 0:lim - (t0 + s)], reads=[xn])
            ar.reset(m0)

        for (t0_, ng_) in tiles(0, tf, G3):
            group_d(t0_, ng_)
        pg.barrier()

    pg.barrier()
    order = []
    stage_ada()
    plan = [("b1", 0), ("b2", 0), ("c", 0), ("d", 0), ("b1", 1), ("b2", 1), ("c", 1), ("d", 1)]
    fns = {"b1": stage_b1, "b2": stage_b2, "c": stage_c, "d": stage_d}
    if upto != "ada":
        for (nm, l) in plan:
            fns[nm](l)
            if upto == f"{nm}{l}":
                break
    if "modfm" in dump:
        md = nc.dram_tensor("modfm_o", [P, NL * 96], F32, kind="ExternalOutput").ap()
        pg.dma("sp", md, modfm.ap.rearrange("p l c -> p (l c)"), reads=[modfm])
    pg.barrier()
    with nc.allow_non_contiguous_dma(reason="single-token (+1 conv halo) columns"):
        pg.emit()
    return nc, es


def _tables(mir):
    T = TKV[0]
    t = np.arange(T)
    pos = (SEQ - 1 - t) if mir else t
    inv = np.float32(500000.0) ** (-np.arange(0, 32, 2, dtype=np.float32) / 32)
    ang = pos.astype(np.float32)[None, :] * inv[:, None]
    ang = np.concatenate([ang, ang], axis=0)
    cs = np.stack([np.cos(ang), np.sin(ang)], axis=1).astype(np.float32)
    wm = np.zeros((P, 2, 3, P), np.float32)
    for ty in range(2):
        q = ty * P + np.arange(P)[None, :]
        for kb in range(3):
            k = kb * P + np.arange(P)[:, None]
            wm[:, ty, kb, :] = np.where(np.abs(q - k) <= 128, 0.0, NEG)
    return cs, wm


def _nbias(rel_bias_l, mir):
    out = np.full((HB, P, 3, 5, P), NEG, np.float32)
    for ty in range(3):
        q = ty * P + np.arange(P)
        for kb in range(5):
            k = kb * P + np.arange(P)
            gq = (SEQ - 1 - q) if mir else q
            gk = (SEQ - 1 - k) if mir else k
            Rq, Cq = gq // 64, gq % 64
            Rk, Ck = gk // 64, gk % 64
            rs = np.clip(Rq - 4, 0, 56)
            c0 = np.clip(Cq - 8, 0, 48)
            ok = ((Rk[:, None] >= rs[None, :]) & (Rk[:, None] < rs[None, :] + 8) &
                  (Ck[:, None] >= c0[None, :]) & (Ck[:, None] < c0[None, :] + 16))
            dr = np.clip(Rk[:, None] - Rq[None, :] + 7, 0, 14)
            dc = np.clip(Ck[:, None] - Cq[None, :] + 15, 0, 30)
            vals = rel_bias_l[:, dr, dc]
            out[:, :, ty, kb, :] = np.where(ok[None], vals, np.float32(NEG))
    return out


def _prep_core(c, x, cvec, norm_mix, norm_ffn, qn_a, kn_a, qn_b, kn_b, sink_a, rel_bias_b, conv_w, conv_b, shared):
    b, mir = c // 2, c % 2
    T = TKV[0]
    if mir:
        xl = x[b, SEQ - T:SEQ][::-1]
    else:
        xl = x[b, 0:T]
    xT = np.ascontiguousarray(xl.T).reshape(DC, P, T)
    sm = np.zeros((P, SM_N), np.float32)
    sm[:, SM_C:SM_C + 16] = cvec[b].reshape(16, P).T
    for l in range(NL):
        o = SM_L + l * SM_LSZ
        sm[:, o + O_NM:o + O_NM + 16] = norm_mix[l].reshape(16, P).T
        sm[:, o + O_NF:o + O_NF + 16] = norm_ffn[l].reshape(16, P).T
        sm[:, o + O_QNA] = qn_a[l]
        sm[:, o + O_KNA] = kn_a[l]
        sm[:, o + O_QNB] = qn_b[l]
        sm[:, o + O_KNB] = kn_b[l]
        sm[:, o + O_SINK:o + O_SINK + 8] = sink_a[l][None, :]
        cw = conv_w[l][::-1] if mir else conv_w[l]
        sm[:, o + O_CW:o + O_CW + 264] = cw.reshape(3, 88, P).transpose(2, 0, 1).reshape(P, 264)
        sm[:, o + O_CB:o + O_CB + 88] = conv_b[l].reshape(88, P).T
    cs, wm = shared["tab"][mir]
    nbias = shared["nb"][mir]
    m = dict(shared["w"])
    m.update({"xin": xT, "sm": sm, "cs": cs, "wmask": wm, "nbias": nbias})
    return m


_CACHE = {}


def _get_prog():
    if "nc" not in _CACHE:
        _CACHE["nc"] = build_program()
    return _CACHE["nc"][0]


def kernel(x, c, ada_w, ada_b, norm_mix, norm_ffn, w_in, qn_a, kn_a, qn_b, kn_b,
           sink_a, rel_bias_b, w_proj_a, w_proj_b, w_out, w_up, conv_w, conv_b, w_down):
    f = lambda a: np.ascontiguousarray(np.asarray(a, dtype=np.float32))
    x, c = f(x), f(c)
    norm_mix, norm_ffn, qn_a, kn_a, qn_b, kn_b = map(f, (norm_mix, norm_ffn, qn_a, kn_a, qn_b, kn_b))
    sink_a, rel_bias_b, conv_w, conv_b = map(f, (sink_a, rel_bias_b, conv_w, conv_b))
    rm = np.zeros((32, 32), np.float32)
    for m_ in range(16):
        rm[m_ + 16, m_] = -1.0
        rm[m_, m_ + 16] = 1.0
    shared = {
        "w": {"ada_w": f(ada_w), "ada_b": f(ada_b), "w_in": f(w_in), "w_proj_a": f(w_proj_a), "w_proj_b": f(w_proj_b),
              "w_out": f(w_out), "w_up": f(w_up), "w_down": f(w_down), "rmat": rm},
        "tab": [_tables(0), _tables(1)],
        "nb": [np.stack([_nbias(rel_bias_b[l], mir) for l in range(NL)]) for mir in range(2)],
    }
    in_maps = [_prep_core(cc, x, c, norm_mix, norm_ffn, qn_a, kn_a, qn_b, kn_b, sink_a, rel_bias_b, conv_w, conv_b, shared)
               for cc in range(8)]
    nc = _get_prog()
    res = run_bass_kernel_spmd(nc, in_maps, core_ids=list(range(8)))
    out = np.empty((4, SEQ, D), np.float32)
    for cc in range(8):
        o = np.asarray(res.results[cc]["out"]).reshape(D, 2048).T
        b, mir = cc // 2, cc % 2
        if mir:
            out[b, 2048:] = o[::-1]
        else:
            out[b, :2048] = o
    return out
```

```python
import numpy as np
from contextlib import ExitStack
import concourse.bass as bass
import concourse.mybir as mybir
from concourse.bass_utils import run_bass_kernel_spmd

F32 = mybir.dt.float32
BF16 = mybir.dt.bfloat16
AF = mybir.ActivationFunctionType
ALU = mybir.AluOpType

P = 128
D = 2048
DC = 16
SEQ = 4096
NL = 2
HA, GA, HB = 8, 2, 8
INC = 8704
DFF = 5632
FC = 44
TKV = [2688, 2368]
TQ = [2369, 2049]
TF = [2368, 2048]
NEG = -1e30
EPS = 1e-6
QA0, KA0, VA0, QB0, KB0, VB0, GA0, GB0 = 0, 1024, 1280, 1536, 2560, 3584, 4608, 6656
SM_C = 0
SM_L = 16
SM_LSZ = 16 + 16 + 4 + 8 + 264 + 88
O_NM, O_NF, O_QNA, O_KNA, O_QNB, O_KNB, O_SINK, O_CW, O_CB = 0, 16, 32, 33, 34, 35, 36, 44, 308
SM_N = SM_L + NL * SM_LSZ
SEM_LIMIT = 30000


def tiles(lo, hi, step):
    t = [(s, min(step, hi - s)) for s in range(lo, hi, step)]
    if len(t) > 1 and t[-1][1] < 128:
        s0 = t[-2][0]
        tot = t[-2][1] + t[-1][1]
        a = (tot + 1) // 2
        t[-2:] = [(s0, a), (s0 + a, tot - a)]
    return t


class Tok:
    __slots__ = ("sid", "sem", "val")

    def __init__(self, sid, sem, val):
        self.sid, self.sem, self.val = sid, sem, val


class Buf:
    __slots__ = ("w", "r", "name")

    def __init__(self, name=""):
        self.w = None
        self.r = {}
        self.name = name


class Tile:
    __slots__ = ("ap", "buf")

    def __init__(self, ap, buf=None, name=""):
        self.ap = ap
        self.buf = buf if buf is not None else Buf(name)


class Eng:
    def __init__(self, prog, name):
        self.prog = prog
        self.name = name
        self.q = []
        self.waited = {}
        self.sem = None
        self.cnt = 0
        self.last = None
        self.ring = []
        self.rk = 0

    def wait(self, tok):
        if tok is None:
            return
        if self.waited.get(tok.sid, 0) >= tok.val:
            return
        self.waited[tok.sid] = tok.val
        self.q.append(("w", tok.sem, tok.val))

    def next_tok(self):
        if self.sem is None or self.cnt >= SEM_LIMIT:
            self.sem = self.prog.new_sem(self.name)
            self.cnt = 0
        self.cnt += 1
        self.last = Tok(self.sem[0], self.sem[1], self.cnt)
        return self.last


class Prog:
    def __init__(self, nc, es):
        self.nc = nc
        self.es = es
        self.nsem = 0
        self.E = {n: Eng(self, n) for n in ("pe", "act", "dve", "pool", "sp")}
        for qn, R in (("sp", 12), ("pool", 12), ("act", 8)):
            e = self.E[qn]
            e.ring = [[self.new_sem(qn + "r"), 0, None] for _ in range(R)]

    def new_sem(self, name):
        self.nsem += 1
        s = self.es.enter_context(self.nc.semaphore(f"{name}{self.nsem}"))
        return (self.nsem, s)

    def _deps(self, e, reads, writes):
        for b in reads:
            e.wait(b.buf.w)
        for b in writes:
            e.wait(b.buf.w)
            for t in b.buf.r.values():
                e.wait(t)

    def _commit(self, tok, reads, writes):
        for b in writes:
            b.buf.w = tok
            b.buf.r = {}
        for b in reads:
            o = b.buf.r.get(tok.sid)
            if o is None or o.val < tok.val:
                b.buf.r[tok.sid] = tok

    def op(self, en, fn, reads=(), writes=()):
        e = self.E[en]
        self._deps(e, reads, writes)
        tok = e.next_tok()
        e.q.append(("o", fn, tok.sem, 1))
        self._commit(tok, reads, writes)
        return tok

    def dma(self, qn, out_ap, in_ap, reads=(), writes=()):
        e = self.E[qn]
        self._deps(e, reads, writes)
        slot = e.ring[e.rk % len(e.ring)]
        e.rk += 1
        e.wait(slot[2])
        slot[1] += 16
        tok = Tok(slot[0][0], slot[0][1], slot[1])
        slot[2] = tok
        e.q.append(("o", (lambda g, o=out_ap, i=in_ap: g.dma_start(out=o, in_=i)), tok.sem, 16))
        self._commit(tok, reads, writes)
        return tok

    def barrier(self):
        toks = []
        for e in self.E.values():
            if e.last is not None:
                toks.append(e.last)
            for s in e.ring:
                if s[2] is not None:
                    toks.append(s[2])
        for e in self.E.values():
            for t in toks:
                e.wait(t)

    def emit(self):
        nc = self.nc
        E = self.E

        def run(e, g):
            for it in e.q:
                if it[0] == "w":
                    g.wait_ge(it[1], it[2])
                else:
                    ins = it[1](g)
                    ins.then_inc(it[2], it[3])

        with nc.Block() as block:
            @block.tensor
            def _(g):
                run(E["pe"], g)

            @block.scalar
            def _(g):
                run(E["act"], g)

            @block.vector
            def _(g):
                run(E["dve"], g)

            @block.gpsimd
            def _(g):
                run(E["pool"], g)

            @block.sync
            def _(g):
                run(E["sp"], g)


class Arena:
    def __init__(self, h32, nbytes):
        self.h32 = h32
        self.h16 = h32.bitcast(BF16)
        self.n = nbytes
        self.top = 0
        self.w32 = nbytes // 4
        self.hist = []

    def mark(self):
        return self.top

    def reset(self, m):
        self.top = m

    def alloc(self, shape, dt, name=""):
        free = 1
        for s in shape[1:]:
            free *= s
        esz = 4 if dt == F32 else 2
        nb = (free * esz + 31) // 32 * 32
        off = self.top
        self.top += nb
        assert self.top <= self.n, f"arena overflow {name} {self.top} > {self.n}"
        h = self.h32 if dt == F32 else self.h16
        o = off // esz
        ap = h[0:shape[0], o:o + free]
        if len(shape) == 3:
            ap = ap.rearrange("p (a b) -> p a b", a=shape[1])
        elif len(shape) == 4:
            ap = ap.rearrange("p (a b c) -> p a b c", a=shape[1], b=shape[2])
        t = Tile(ap, name=name)
        end = off + nb
        keep = []
        for (o0, o1, ob) in self.hist:
            if o0 < end and off < o1:
                toks = list(ob.r.values())
                if ob.w is not None:
                    toks.append(ob.w)
                for tk in toks:
                    cur = t.buf.r.get(tk.sid)
                    if cur is None or cur.val < tk.val:
                        t.buf.r[tk.sid] = tk
                if o0 >= off and o1 <= end:
                    continue
            keep.append((o0, o1, ob))
        keep.append((off, end, t.buf))
        self.hist = keep
        return t

    def bcast_mid(self, t, nrep, n):
        a = t.ap
        return bass.AP(a.tensor, a.offset, [[self.w32, a.shape[0]], [0, nrep], [1, n]])


class Ring:
    def __init__(self, items):
        self.items = items
        self.k = 0

    def next(self):
        t = self.items[self.k % len(self.items)]
        self.k += 1
        return t


def build_program(upto="all", dump=()):
    nc = bass.Bass("TRN2", target_bir_lowering=False)
    es = ExitStack()

    def din(name, shape, dt=F32):
        return nc.dram_tensor(name, list(shape), dt, kind="ExternalInput").ap()

    def dscr(name, shape, dt):
        kind = "ExternalOutput" if name in dump else "Internal"
        return nc.dram_tensor(name, list(shape), dt, kind=kind).ap()

    xin = din("xin", [DC, P, TKV[0]])
    sm_d = din("sm", [P, SM_N])
    ada_w = din("ada_w", [NL, D, 6 * D])
    ada_b = din("ada_b", [NL, 6 * D])
    w_in = din("w_in", [NL, D, INC])
    w_pa = din("w_proj_a", [NL, 1024, D])
    w_pb = din("w_proj_b", [NL, 1024, D])
    w_out = din("w_out", [NL, D, D])
    w_up = din("w_up", [NL, D, 2 * DFF])
    w_down = din("w_down", [NL, DFF, D])
    cs_d = din("cs", [32, 2, TKV[0]])
    wm_d = din("wmask", [P, 2, 3, P])
    nb_d = din("nbias", [NL, HB, P, 3, 5, P])
    rm_d = din("rmat", [32, 32])
    out_d = nc.dram_tensor("out", [DC, P, 2048], F32, kind="ExternalOutput").ap()

    xmid = [dscr(f"xmid{l}", [DC, P, TQ[l]], F32) for l in range(NL)]
    x1 = dscr("x1", [DC, P, TF[0]], F32)
    xsrc = [xin, x1]
    xdst = [x1, out_d]
    NKB = [(TKV[l] + 127) // 128 for l in range(NL)]
    qs = [dscr(f"qs{l}", [16, P, TQ[l]], BF16) for l in range(NL)]
    ks = [dscr(f"ks{l}", [10, P, TKV[l]], BF16) for l in range(NL)]
    vs = [dscr(f"vs{l}", [10, P, NKB[l], P], BF16) for l in range(NL)]
    gs = [dscr(f"gs{l}", [32, P, TQ[l]], BF16) for l in range(NL)]
    os_ = [dscr(f"os{l}", [16, P, TQ[l]], BF16) for l in range(NL)]
    a_s = None

    pg = Prog(nc, es)
    ARENA_BYTES = 206 * 1024
    h32 = es.enter_context(nc.sbuf_tensor("arena", [P, ARENA_BYTES // 4], F32))
    ar = Arena(h32, ARENA_BYTES)
    psum = es.enter_context(nc.psum_tensor("psum", [P, 4096], F32))
    banks = [Tile(psum[:, i * 512:(i + 1) * 512], name=f"bank{i}") for i in range(8)]

    sm = ar.alloc([P, SM_N], F32, "sm")
    modfm = ar.alloc([P, NL, 96], F32, "modfm")
    der = ar.alloc([P, NL, 6, 16], F32, "der")
    ones_bf = ar.alloc([P, P], BF16, "ones_bf")
    ones_f = ar.alloc([P, P], F32, "ones_f")
    rmat = ar.alloc([32, 32], F32, "rmat")
    wmask = ar.alloc([P, 2, 3, P], F32, "wmask")
    cact = ar.alloc([P, 16], BF16, "cact")
    qsc = ar.alloc([P, NL, 2], F32, "qsc")
    esink = ar.alloc([P, NL, 8], F32, "esink")
    base_mark = ar.mark()

    def smc(l, off, n=1):
        c0 = SM_L + l * SM_LSZ + off
        return sm.ap[:, c0:c0 + n]

    pg.dma("sp", sm.ap, sm_d, writes=[sm])
    pg.dma("sp", rmat.ap, rm_d, writes=[rmat])
    pg.dma("sp", wmask.ap, wm_d, writes=[wmask])
    pg.op("dve", lambda g: g.memset(ones_bf.ap, 1.0), writes=[ones_bf])
    pg.op("dve", lambda g: g.memset(ones_f.ap, 1.0), writes=[ones_f])
    pg.op("act", lambda g: g.activation(out=cact.ap, in_=sm.ap[:, SM_C:SM_C + 16], func=AF.Silu),
          reads=[sm], writes=[cact])
    for l in range(NL):
        pg.op("dve", lambda g, l=l: g.tensor_scalar(qsc.ap[:, l, 0:1], smc(l, O_QNA), float(128 ** -0.5), None, ALU.mult),
              reads=[sm], writes=[qsc])
        pg.op("dve", lambda g, l=l: g.tensor_scalar(qsc.ap[:, l, 1:2], smc(l, O_QNB), float(128 ** -0.5), None, ALU.mult),
              reads=[sm], writes=[qsc])
        pg.op("act", lambda g, l=l: g.activation(out=esink.ap[:, l, :], in_=smc(l, O_SINK, 8), func=AF.Exp),
              reads=[sm], writes=[esink])

    def ada_layer(l, wr, br, rr, pr, p2):
        adaw = {}

        def ada_load(g_):
            if g_ >= 24:
                return
            wt_ = wr.next()
            pg.dma("pool", wt_.ap, ada_w[l, :, g_ * 512:(g_ + 1) * 512].rearrange("(k p) n -> p k n", p=P), writes=[wt_])
            adaw[g_] = wt_
        ada_load(0)
        for g in range(24):
            ada_load(g + 1)
            wt = adaw.pop(g)
            bt = br.next()
            c0 = g * 512
            pg.dma("sp", bt.ap, ada_b[l:l + 1, c0:c0 + 512], writes=[bt])
            pa = pr.next()

            def f(g_, wt=wt, pa=pa):
                for k in range(DC):
                    ins = g_.matmul(pa.ap[0:1, :], cact.ap[:, k:k + 1], wt.ap[:, k, :], start=(k == 0), stop=(k == DC - 1))
                return ins
            pg.op("pe", f, reads=[wt, cact], writes=[pa])
            row = rr.next()
            pg.op("dve", lambda g_, row=row, pa=pa, bt=bt: g_.tensor_tensor(row.ap, pa.ap[0:1, :], bt.ap, ALU.add),
                  reads=[pa, bt], writes=[row])
            yield
            pb = p2.next()

            def f2(g_, row=row, pb=pb):
                for s_ in range(4):
                    ins = g_.matmul(pb.ap[:, s_:s_ + 1], row.ap[0:1, s_ * P:(s_ + 1) * P], ones_f.ap[0:1, 0:1], start=True, stop=True)
                return ins
            pg.op("pe", f2, reads=[row, ones_f], writes=[pb])
            pg.op("act", lambda g_, pb=pb, g=g: g_.copy(modfm.ap[:, l, g * 4:(g + 1) * 4], pb.ap[:, 0:4]),
                  reads=[pb], writes=[modfm])
        for s_, (isc, ish, igt, ogam) in enumerate(((16, 0, 32, O_NM), (64, 48, 80, O_NF))):
            pg.op("dve", lambda g_, s_=s_, isc=isc, ogam=ogam: g_.scalar_tensor_tensor(
                der.ap[:, l, 3 * s_ + 0, :], modfm.ap[:, l, isc:isc + 16], 1.0, smc(l, ogam, 16), ALU.add, ALU.mult),
                reads=[modfm, sm], writes=[der])
            pg.op("dve", lambda g_, s_=s_, ish=ish: g_.tensor_copy(der.ap[:, l, 3 * s_ + 1, :], modfm.ap[:, l, ish:ish + 16]),
                  reads=[modfm], writes=[der])
            pg.op("dve", lambda g_, s_=s_, igt=igt: g_.tensor_copy(der.ap[:, l, 3 * s_ + 2, :], modfm.ap[:, l, igt:igt + 16]),
                  reads=[modfm], writes=[der])

    def ada_rings():
        wr = Ring([ar.alloc([P, DC, 512], BF16, f"adaw{i}") for i in range(2)])
        br = Ring([ar.alloc([1, 512], F32, f"adab{i}") for i in range(2)])
        rr = Ring([ar.alloc([1, 512], F32, f"adar{i}") for i in range(3)])
        return wr, br, rr

    def stage_ada():
        m0 = ar.mark()
        wr, br, rr = ada_rings()
        for _ in ada_layer(0, wr, br, rr, Ring(banks[0:2]), Ring(banks[2:4])):
            pass
        pg.barrier()
        ar.reset(m0)

    def stage_norm(xd, lo, hi, hT, l, which, NT=512, depth=2):
        m0 = ar.mark()
        H = DC // 2
        xr = Ring([(ar.alloc([P, H, NT], F32, f"nxa{i}"), ar.alloc([P, H, NT], F32, f"nxb{i}")) for i in range(depth)])
        sq = Ring([ar.alloc([P, DC, NT], BF16, f"nsq{i}") for i in range(depth)])
        rs = Ring([ar.alloc([P, NT], F32, f"nrs{i}") for i in range(depth)])
        pr = Ring(banks[0:3])
        A = der.ap[:, l, 3 * which + 0, :]
        B = der.ap[:, l, 3 * which + 1, :]
        def ntile(s, n):
            xa, xb = xr.next()
            pg.dma("sp", xa.ap[:, :, 0:n], xd[0:H, :, s:s + n].rearrange("c p t -> p c t"), writes=[xa])
            pg.dma("pool", xb.ap[:, :, 0:n], xd[H:DC, :, s:s + n].rearrange("c p t -> p c t"), writes=[xb])
            st = sq.next()
            pg.op("act", lambda g: g.activation(out=st.ap[:, 0:H, 0:n], in_=xa.ap[:, :, 0:n], func=AF.Square),
                  reads=[xa], writes=[st])
            pg.op("act", lambda g: g.activation(out=st.ap[:, H:DC, 0:n], in_=xb.ap[:, :, 0:n], func=AF.Square),
                  reads=[xb], writes=[st])
            pa = pr.next()

            def f(g):
                for k in range(DC):
                    ins = g.matmul(pa.ap[:, 0:n], ones_bf.ap, st.ap[:, k, 0:n], start=(k == 0), stop=(k == DC - 1))
                return ins
            pg.op("pe", f, reads=[st, ones_bf], writes=[pa])
            yield
            r = rs.next()
            pg.op("act", lambda g: g.activation(out=r.ap[:, 0:n], in_=pa.ap[:, 0:n], func=AF.Ln, bias=EPS, scale=1.0 / D),
                  reads=[pa], writes=[r])
            pg.op("act", lambda g: g.activation(out=r.ap[:, 0:n], in_=r.ap[:, 0:n], func=AF.Exp, scale=-0.5), reads=[r], writes=[r])
            for xh in (xa, xb):
                pg.op("dve", lambda g, xh=xh: g.tensor_tensor(xh.ap[:, :, 0:n], xh.ap[:, :, 0:n], ar.bcast_mid(Tile(r.ap[:, 0:n]), H, n), ALU.mult),
                      reads=[xh, r], writes=[xh])

            def f3(g):
                for k in range(DC):
                    src = xa if k < H else xb
                    ins = g.activation(out=hT.ap[:, k, s - lo:s - lo + n], in_=src.ap[:, k % H, 0:n], func=AF.Identity,
                                       bias=B[:, k:k + 1], scale=A[:, k:k + 1])
                return ins
            pg.op("act", f3, reads=[xa, xb, der], writes=[hT])

        prev = None
        for (s_, n_) in tiles(lo, hi, NT):
            cur = ntile(s_, n_)
            next(cur)
            if prev is not None:
                for _ in prev:
                    pass
            prev = cur
        for _ in prev:
            pass
        ar.reset(m0)

    def stage_b1(l):
        m0 = ar.mark()
        tkv, tq = TKV[l], TQ[l]
        hT = ar.alloc([P, DC, tkv], BF16, "hT")
        stage_norm(xsrc[l], 0, tkv, hT, l, 0)
        cs = ar.alloc([32, 2, TKV[0]], F32, "cs")
        pg.dma("sp", cs.ap, cs_d, writes=[cs])
        wr = Ring([ar.alloc([P, DC, 512], BF16, f"b1w{i}") for i in range(2)])
        sqr = Ring([ar.alloc([P, 512], BF16, f"b1sq{i}") for i in range(3)])
        rsr = Ring([ar.alloc([P, 512], F32, f"b1rs{i}") for i in range(3)])
        obr = Ring([ar.alloc([P, 512], BF16, f"b1ob{i}") for i in range(6)])
        t32r = Ring([ar.alloc([32, 512], F32, f"b1t{i}") for i in range(4)])
        o1r = Ring([ar.alloc([32, 512], F32, f"b1o1{i}") for i in range(3)])
        o2r = Ring([ar.alloc([32, 512], F32, f"b1o2{i}") for i in range(3)])
        pmain = Ring(banks[0:4])
        pss = Ring(banks[4:6])
        prot = Ring(banks[6:8])

        def mm_fm(wt, ci, s, n):
            pa = pmain.next()

            def f(g, pa=pa):
                for k in range(DC):
                    ins = g.matmul(pa.ap[:, 0:n], wt.ap[:, k, ci * P:(ci + 1) * P], hT.ap[:, k, s:s + n],
                                   start=(k == 0), stop=(k == DC - 1))
                return ins
            pg.op("pe", f, reads=[wt, hT], writes=[pa])
            return pa

        pending = []

        def pump():
            for g_ in list(pending):
                try:
                    next(g_)
                except StopIteration:
                    pending.remove(g_)

        def launch(gen):
            pump()
            try:
                next(gen)
                pending.append(gen)
            except StopIteration:
                pass

        def epi_qk(pa, s, n, gain, rot, dst):
            sq = sqr.next()
            pg.op("act", lambda g: g.activation(out=sq.ap[:, 0:n], in_=pa.ap[:, 0:n], func=AF.Square), reads=[pa], writes=[sq])
            yield
            ps = pss.next()
            pg.op("pe", lambda g: g.matmul(ps.ap[:, 0:n], ones_bf.ap, sq.ap[:, 0:n], start=True, stop=True),
                  reads=[sq, ones_bf], writes=[ps])
            r = rsr.next()
            pg.op("act", lambda g: g.activation(out=r.ap[:, 0:n], in_=ps.ap[:, 0:n], func=AF.Ln, bias=EPS, scale=1.0 / 128),
                  reads=[ps], writes=[r])
            pg.op("act", lambda g: g.activation(out=r.ap[:, 0:n], in_=r.ap[:, 0:n], func=AF.Exp, scale=-0.5),
                  reads=[r], writes=[r])
            ob = obr.next()
            pg.op("dve", lambda g: g.scalar_tensor_tensor(ob.ap[:, 0:n], pa.ap[:, 0:n], gain, r.ap[:, 0:n], ALU.mult, ALU.mult),
                  reads=[pa, r, sm, qsc], writes=[ob])
            if rot:
                t32 = t32r.next()
                pg.op("dve", lambda g: g.scalar_tensor_tensor(t32.ap[:, 0:n], pa.ap[0:32, 0:n], gain[0:32, :], r.ap[0:32, 0:n], ALU.mult, ALU.mult),
                      reads=[pa, r, sm, qsc], writes=[t32])
                yield
                yield
                pr = prot.next()
                pg.op("pe", lambda g: g.matmul(pr.ap[0:32, 0:n], rmat.ap, t32.ap[:, 0:n], start=True, stop=True),
                      reads=[t32, rmat], writes=[pr])
                o1 = o1r.next()
                pg.op("pool", lambda g: g.tensor_tensor(o1.ap[:, 0:n], t32.ap[:, 0:n], cs.ap[:, 0, s:s + n], ALU.mult),
                      reads=[t32, cs], writes=[o1])
                o2 = o2r.next()
                pg.op("dve", lambda g: g.tensor_tensor(o2.ap[:, 0:n], pr.ap[0:32, 0:n], cs.ap[:, 1, s:s + n], ALU.mult),
                      reads=[pr, cs], writes=[o2])
                pg.op("dve", lambda g: g.tensor_tensor(ob.ap[0:32, 0:n], o1.ap[:, 0:n], o2.ap[:, 0:n], ALU.add),
                      reads=[o1, o2], writes=[ob])
            pg.dma("sp", dst[:, s:s + n], ob.ap[:, 0:n], reads=[ob])

        def epi_gate(pa, s, n, dst):
            ob = obr.next()
            pg.op("act", lambda g: g.activation(out=ob.ap[:, 0:n], in_=pa.ap[:, 0:n], func=AF.Sigmoid), reads=[pa], writes=[ob])
            pg.dma("sp", dst[:, s:s + n], ob.ap[:, 0:n], reads=[ob])

        def do_v(wt, c0, ncols, h0):
            nh = ncols // P
            for tb in range(NKB[l]):
                t0 = tb * P
                nt = min(P, tkv - t0)
                pa = pmain.next()

                def f(g, pa=pa, t0=t0, nt=nt):
                    for k in range(DC):
                        ins = g.matmul(pa.ap[0:nt, 0:ncols], hT.ap[:, k, t0:t0 + nt], wt.ap[:, k, c0:c0 + ncols],
                                       start=(k == 0), stop=(k == DC - 1))
                    return ins
                pg.op("pe", f, reads=[wt, hT], writes=[pa])
                pump()
                ob = obr.next()
                pg.op("act", lambda g, ob=ob, pa=pa, nt=nt: g.copy(ob.ap[0:nt, 0:ncols], pa.ap[0:nt, 0:ncols]), reads=[pa], writes=[ob])
                pg.dma("sp", vs[l][h0:h0 + nh, 0:nt, tb, :].rearrange("h k d -> k h d"),
                       ob.ap[0:nt, 0:ncols].rearrange("k (h d) -> k h d", h=nh), reads=[ob])

        qnA, knA = qsc.ap[:, l, 0:1], smc(l, O_KNA)
        qnB, knB = qsc.ap[:, l, 1:2], smc(l, O_KNB)
        b1w = {}

        def b1_load(grp_):
            if grp_ >= 17:
                return
            wt_ = wr.next()
            pg.dma("pool", wt_.ap, w_in[l, :, grp_ * 512:(grp_ + 1) * 512].rearrange("(k p) n -> p k n", p=P), writes=[wt_])
            b1w[grp_] = wt_
        b1_load(0)
        for grp in range(17):
            b1_load(grp + 1)
            wt = b1w.pop(grp)
            c0 = grp * 512
            for ci in range(4):
                col = c0 + ci * P
                if col < KA0:
                    h = col // P
                    for (s, n) in tiles(0, tq, 512):
                        launch(epi_qk(mm_fm(wt, ci, s, n), s, n, qnA, True, qs[l][h]))
                elif col < VA0:
                    gidx = (col - KA0) // P
                    for (s, n) in tiles(0, tkv, 512):
                        launch(epi_qk(mm_fm(wt, ci, s, n), s, n, knA, True, ks[l][gidx]))
                elif col < QB0:
                    if col == VA0:
                        do_v(wt, ci * P, 256, 0)
                elif col < KB0:
                    h = (col - QB0) // P
                    for (s, n) in tiles(0, tq, 512):
                        launch(epi_qk(mm_fm(wt, ci, s, n), s, n, qnB, False, qs[l][8 + h]))
                elif col < VB0:
                    h = (col - KB0) // P
                    for (s, n) in tiles(0, tkv, 512):
                        launch(epi_qk(mm_fm(wt, ci, s, n), s, n, knB, False, ks[l][2 + h]))
                elif col < GA0:
                    if ci == 0:
                        do_v(wt, 0, 512, 2 + (col - VB0) // P)
                else:
                    gi = (col - GA0) // P
                    for (s, n) in tiles(0, tq, 512):
                        epi_gate(mm_fm(wt, ci, s, n), s, n, gs[l][gi])
                        pump()
        while pending:
            pump()
        pg.barrier()
        ar.reset(m0)

    def stage_b2(l):
        m0 = ar.mark()
        tkv, tq, nkb_all = TKV[l], TQ[l], NKB[l]
        kpad = nkb_all * P
        qr = Ring([ar.alloc([P, tq], BF16, f"q{i}") for i in range(2)])
        kr = Ring([ar.alloc([P, kpad], BF16, f"k{i}") for i in range(2)])
        vr = Ring([ar.alloc([P, nkb_all, P], BF16, f"v{i}") for i in range(2)])
        orr = Ring([ar.alloc([P, tq], BF16, f"o{i}") for i in range(2)])
        nbr = Ring([ar.alloc([P, 3, 5, P], F32, f"nb{i}") for i in range(2)])
        tr = Ring([ar.alloc([P, 5, P], F32, f"t{i}") for i in range(3)])
        prr = Ring([ar.alloc([P, 5, P], BF16, f"p{i}") for i in range(3)])
        rir = Ring([ar.alloc([P, P], F32, f"ri{i}") for i in range(3)])
        ps_s = Ring([Tile(psum[:, 0:1024], name="S0"), Tile(psum[:, 1024:2048], name="S1")])
        adagen = None
        if l == 0:
            ps_o = Ring(banks[4:7])
            awr, abr, arr = ada_rings()
            adagen = ada_layer(1, awr, abr, arr, Ring([banks[7]]), Ring([banks[7]]))
        else:
            ps_o = Ring(banks[4:8])

        def ada_step(k):
            if adagen is not None:
                for _ in range(k):
                    next(adagen, None)
        if kpad > tkv:
            for t in kr.items:
                pg.op("dve", lambda g, t=t: g.memset(t.ap[:, tkv:kpad], 0.0), writes=[t])
            rem = tkv - (nkb_all - 1) * P
            for t in vr.items:
                pg.op("dve", lambda g, t=t, rem=rem: g.memset(t.ap[rem:P, nkb_all - 1, :], 0.0), writes=[t])

        def load_kv(idx):
            kt, vt = kr.next(), vr.next()
            pg.dma("sp", kt.ap[:, 0:tkv], ks[l][idx], writes=[kt])
            full = tkv // P
            pg.dma("sp", vt.ap[:, 0:full, :], vs[l][idx, :, 0:full, :], writes=[vt])
            if full < nkb_all:
                rem = tkv - full * P
                pg.dma("sp", vt.ap[0:rem, full, :], vs[l][idx, 0:rem, full, :], writes=[vt])
            return kt, vt

        def head(hq, kt, vt, nkb, tab, is_win):
            qt = qr.next()
            pg.dma("sp", qt.ap, qs[l][hq], writes=[qt])
            ot = orr.next()
            nqb = (tq + P - 1) // P
            def qblock(i):
                q0 = i * P
                nq = min(P, tq - q0)
                if is_win:
                    ty, kb0 = min(i, 1), max(i - 1, 0)
                else:
                    ty, kb0 = min(i, 2), max(i - 2, 0)
                S = ps_s.next()
                S3 = S.ap[:, 0:nkb * P].rearrange("p (j q) -> p j q", j=nkb)

                def f(g):
                    for j in range(nkb):
                        ins = g.matmul(S3[:, j, 0:nq], kt.ap[:, (kb0 + j) * P:(kb0 + j + 1) * P], qt.ap[:, q0:q0 + nq],
                                       start=True, stop=True)
                    return ins
                pg.op("pe", f, reads=[kt, qt], writes=[S])
                T_ = tr.next()
                pg.op("dve", lambda g: g.tensor_tensor(T_.ap[:, 0:nkb, 0:nq], S3[:, :, 0:nq], tab.ap[:, ty, 0:nkb, 0:nq], ALU.add),
                      reads=[S, tab], writes=[T_])
                Pm = prr.next()
                pg.op("act", lambda g: g.activation(out=Pm.ap[:, 0:nkb, 0:nq], in_=T_.ap[:, 0:nkb, 0:nq], func=AF.Exp),
                      reads=[T_], writes=[Pm])
                yield
                O = ps_o.next()

                def f2(g):
                    for j in range(nkb):
                        g.matmul(O.ap[:, 0:nq], vt.ap[:, kb0 + j, :], Pm.ap[:, j, 0:nq], start=(j == 0), stop=(j == nkb - 1))
                    for j in range(nkb):
                        ins = g.matmul(O.ap[:, P:P + nq], ones_bf.ap, Pm.ap[:, j, 0:nq], start=(j == 0), stop=(j == nkb - 1))
                    return ins
                pg.op("pe", f2, reads=[vt, Pm, ones_bf], writes=[O])
                ri = rir.next()
                bias_ = esink.ap[:, l, hq:hq + 1] if is_win else 0.0
                pg.op("act", lambda g: g.activation(out=ri.ap[:, 0:nq], in_=O.ap[:, P:P + nq], func=AF.Ln, bias=bias_),
                      reads=[O, esink], writes=[ri])
                pg.op("act", lambda g: g.activation(out=ri.ap[:, 0:nq], in_=ri.ap[:, 0:nq], func=AF.Exp, scale=-1.0),
                      reads=[ri], writes=[ri])
                pg.op("dve", lambda g: g.tensor_tensor(ot.ap[:, q0:q0 + nq], O.ap[:, 0:nq], ri.ap[:, 0:nq], ALU.mult),
                      reads=[O, ri], writes=[ot])

            prev = None
            for i in range(nqb):
                cur = qblock(i)
                next(cur)
                if prev is not None:
                    for _ in prev:
                        pass
                prev = cur
            for _ in prev:
                pass
            return ot

        q2r = Ring([ar.alloc([P, 2, tq], BF16, f"q2{i}") for i in range(2)])
        o2r = Ring([ar.alloc([P, 2, tq], BF16, f"o2{i}") for i in range(2)])
        t2r = Ring([ar.alloc([P, 3, 2, P], F32, f"t2{i}") for i in range(3)])
        p2r = Ring([ar.alloc([P, 3, 2, P], BF16, f"p2{i}") for i in range(3)])
        r2r = Ring([ar.alloc([P, 2, P], F32, f"r2{i}") for i in range(3)])
        e2r = Ring([ar.alloc([P, 2, P], F32, f"e2{i}") for i in range(2)])
        pstep = psum[:, 0:512].ap[0][0]

        def head_pair(h0, kt, vt):
            qt = q2r.next()
            pg.dma("sp", qt.ap, qs[l][h0:h0 + 2].rearrange("h p t -> p h t"), writes=[qt])
            ot = o2r.next()
            es = e2r.next()
            for r_ in range(2):
                pg.op("act", lambda g, r_=r_: g.activation(out=es.ap[:, r_, :], in_=ones_f.ap, func=AF.Identity,
                                                          scale=esink.ap[:, l, h0 + r_:h0 + r_ + 1]),
                      reads=[ones_f, esink], writes=[es])
            nqb = (tq + P - 1) // P

            def qblock(i):
                q0 = i * P
                nq = min(P, tq - q0)
                ty, kb0 = min(i, 1), max(i - 1, 0)
                S = ps_s.next()

                def f(g):
                    for j in range(3):
                        ins = g.matmul(S.ap[:, j * 256:j * 256 + 2 * nq].rearrange("p (h q) -> p h q", h=2),
                                       kt.ap[:, (kb0 + j) * P:(kb0 + j + 1) * P], qt.ap[:, :, q0:q0 + nq], start=True, stop=True)
                    return ins
                pg.op("pe", f, reads=[kt, qt], writes=[S])
                T_ = t2r.next()
                s_in = bass.AP(S.ap.tensor, S.ap.offset, [[pstep, P], [256, 3], [nq, 2], [1, nq]])
                m_in = bass.AP(wmask.ap.tensor, wmask.ap.offset + ty * 3 * P, [[ar.w32, P], [P, 3], [0, 2], [1, nq]])
                pg.op("dve", lambda g: g.tensor_tensor(T_.ap[:, :, :, 0:nq], s_in, m_in, ALU.add), reads=[S, wmask], writes=[T_])
                Pm = p2r.next()
                pg.op("act", lambda g: g.activation(out=Pm.ap[:, :, :, 0:nq], in_=T_.ap[:, :, :, 0:nq], func=AF.Exp),
                      reads=[T_], writes=[Pm])
                yield
                O = ps_o.next()
                o_v = O.ap[:, 0:2 * nq].rearrange("p (h q) -> p h q", h=2)
                s_v = O.ap[:, 256:256 + 2 * nq].rearrange("p (h q) -> p h q", h=2)

                def f2(g):
                    for j in range(3):
                        g.matmul(o_v, vt.ap[:, kb0 + j, :], Pm.ap[:, j, :, 0:nq], start=(j == 0), stop=(j == 2))
                    for j in range(3):
                        ins = g.matmul(s_v, ones_bf.ap, Pm.ap[:, j, :, 0:nq], start=(j == 0), stop=(j == 2))
                    return ins
                pg.op("pe", f2, reads=[vt, Pm, ones_bf], writes=[O])
                ri = r2r.next()
                pg.op("dve", lambda g: g.tensor_tensor(ri.ap[:, :, 0:nq], s_v, es.ap[:, :, 0:nq], ALU.add), reads=[O, es], writes=[ri])
                pg.op("act", lambda g: g.activation(out=ri.ap[:, :, 0:nq], in_=ri.ap[:, :, 0:nq], func=AF.Ln), reads=[ri], writes=[ri])
                pg.op("act", lambda g: g.activation(out=ri.ap[:, :, 0:nq], in_=ri.ap[:, :, 0:nq], func=AF.Exp, scale=-1.0), reads=[ri], writes=[ri])
                pg.op("dve", lambda g: g.tensor_tensor(ot.ap[:, :, q0:q0 + nq], o_v, ri.ap[:, :, 0:nq], ALU.mult),
                      reads=[O, ri], writes=[ot])

            prev = None
            for i in range(nqb):
                cur = qblock(i)
                next(cur)
                if prev is not None:
                    for _ in prev:
                        pass
                prev = cur
            for _ in prev:
                pass
            return ot

        for gi in range(GA):
            kt, vt = load_kv(gi)
            for r in range(0, HA // GA, 2):
                hq = gi * (HA // GA) + r
                ot = head_pair(hq, kt, vt)
                pg.dma("pool", os_[l][hq:hq + 2].rearrange("h p t -> p h t"), ot.ap, reads=[ot])
                ada_step(2)
        for hb in range(HB):
            kt, vt = load_kv(2 + hb)
            tab = nbr.next()
            pg.dma("sp", tab.ap, nb_d[l, hb], writes=[tab])
            ot = head(8 + hb, kt, vt, 5, tab, False)
            pg.dma("pool", os_[l][8 + hb], ot.ap, reads=[ot])
            ada_step(2)
        ada_step(100)
        pg.barrier()
        ar.reset(m0)

    def stage_c(l):
        tq = TQ[l]
        half = (tq // 2 + 127) // 128 * 128
        def group_c(t0, ng):
            m0 = ar.mark()
            O = ar.alloc([P, 16, ng], BF16, "O")
            mg = ar.alloc([P, DC, ng], BF16, "mg")
            pg.dma("sp", O.ap, os_[l][:, :, t0:t0 + ng].rearrange("h p t -> p h t"), writes=[O])
            m1 = ar.mark()
            wa = Ring([ar.alloc([P, 8, 512], BF16, f"wa{i}") for i in range(2)])
            wb = Ring([ar.alloc([P, 8, 512], BF16, f"wb{i}") for i in range(2)])
            sga = Ring([ar.alloc([P, ng], BF16, f"sga{i}") for i in range(2)])
            sgb = Ring([ar.alloc([P, ng], BF16, f"sgb{i}") for i in range(2)])
            m1r = Ring([ar.alloc([P, 512], F32, f"m1{i}") for i in range(2)])
            m2r = Ring([ar.alloc([P, 512], F32, f"m2{i}") for i in range(2)])
            pA = Ring(banks[0:4])
            pB = Ring(banks[4:8])
            c1w = {}

            def c1_load(cg_):
                if cg_ >= 4:
                    return
                a_, b_ = wa.next(), wb.next()
                pg.dma("pool", a_.ap, w_pa[l, :, cg_ * 512:(cg_ + 1) * 512].rearrange("(k p) n -> p k n", p=P), writes=[a_])
                pg.dma("pool", b_.ap, w_pb[l, :, cg_ * 512:(cg_ + 1) * 512].rearrange("(k p) n -> p k n", p=P), writes=[b_])
                c1w[cg_] = (a_, b_)
            c1_load(0)
            for cg in range(4):
                c1_load(cg + 1)
                wta, wtb = c1w.pop(cg)
                c0 = cg * 512
                for ci in range(4):
                    j = cg * 4 + ci
                    ga_t, gb_t = sga.next(), sgb.next()
                    pg.dma("sp", ga_t.ap, gs[l][j, :, t0:t0 + ng], writes=[ga_t])
                    pg.dma("sp", gb_t.ap, gs[l][16 + j, :, t0:t0 + ng], writes=[gb_t])
                    for (s, n) in tiles(0, ng, 512):
                        pa, pb = pA.next(), pB.next()

                        def f(g, pa=pa, pb=pb, wta=wta, wtb=wtb, ci=ci, s=s, n=n):
                            for k in range(8):
                                g.matmul(pa.ap[:, 0:n], wta.ap[:, k, ci * P:(ci + 1) * P], O.ap[:, k, s:s + n], start=(k == 0), stop=(k == 7))
                            for k in range(8):
                                ins = g.matmul(pb.ap[:, 0:n], wtb.ap[:, k, ci * P:(ci + 1) * P], O.ap[:, 8 + k, s:s + n], start=(k == 0), stop=(k == 7))
                            return ins
                        pg.op("pe", f, reads=[wta, wtb, O], writes=[pa, pb])
                        t1, t2 = m1r.next(), m2r.next()
                        pg.op("dve", lambda g, t1=t1, pa=pa, ga_t=ga_t, s=s, n=n: g.tensor_tensor(t1.ap[:, 0:n], pa.ap[:, 0:n], ga_t.ap[:, s:s + n], ALU.mult),
                              reads=[pa, ga_t], writes=[t1])
                        pg.op("dve", lambda g, t2=t2, pb=pb, gb_t=gb_t, s=s, n=n: g.tensor_tensor(t2.ap[:, 0:n], pb.ap[:, 0:n], gb_t.ap[:, s:s + n], ALU.mult),
                              reads=[pb, gb_t], writes=[t2])
                        pg.op("dve", lambda g, t1=t1, t2=t2, j=j, s=s, n=n: g.tensor_tensor(mg.ap[:, j, s:s + n], t1.ap[:, 0:n], t2.ap[:, 0:n], ALU.add),
                              reads=[t1, t2], writes=[mg])
            ar.reset(m1)
            wo = Ring([ar.alloc([P, DC, 512], BF16, f"wo{i}") for i in range(2)])
            xr = Ring([ar.alloc([P, 512], F32, f"cx{i}") for i in range(3)])
            xo = Ring([ar.alloc([P, 512], F32, f"cxo{i}") for i in range(3)])
            pA = Ring(banks[0:8])
            G = der.ap[:, l, 2, :]
            c2w = {}

            def c2_load(cg_):
                if cg_ >= 4:
                    return
                w_ = wo.next()
                pg.dma("pool", w_.ap, w_out[l, :, cg_ * 512:(cg_ + 1) * 512].rearrange("(k p) n -> p k n", p=P), writes=[w_])
                c2w[cg_] = w_
            c2_load(0)
            for cg in range(4):
                c2_load(cg + 1)
                wt = c2w.pop(cg)
                c0 = cg * 512
                for ci in range(4):
                    j = cg * 4 + ci
                    for (s, n) in tiles(0, ng, 512):
                        xt = xr.next()
                        pg.dma("sp", xt.ap[:, 0:n], xsrc[l][j, :, t0 + s:t0 + s + n], writes=[xt])
                        pa = pA.next()

                        def f(g, pa=pa, wt=wt, ci=ci, s=s, n=n):
                            for k in range(DC):
                                ins = g.matmul(pa.ap[:, 0:n], wt.ap[:, k, ci * P:(ci + 1) * P], mg.ap[:, k, s:s + n], start=(k == 0), stop=(k == DC - 1))
                            return ins
                        pg.op("pe", f, reads=[wt, mg], writes=[pa])
                        xn = xo.next()
                        pg.op("dve", lambda g, xn=xn, pa=pa, xt=xt, j=j, n=n: g.scalar_tensor_tensor(xn.ap[:, 0:n], pa.ap[:, 0:n], G[:, j:j + 1], xt.ap[:, 0:n], ALU.mult, ALU.add),
                              reads=[pa, xt, der], writes=[xn])
                        pg.dma("act", xmid[l][j, :, t0 + s:t0 + s + n], xn.ap[:, 0:n], reads=[xn])
            ar.reset(m0)

        for (t0_, ng_) in tiles(0, tq, half):
            group_c(t0_, ng_)
        pg.barrier()

    def stage_d(l):
        tf = TF[l]
        G3 = (tf // 3 + 63) // 64 * 64
        Gm = der.ap[:, l, 5, :]
        def group_d(t0, ng):
            m0 = ar.mark()
            aT = ar.alloc([P, FC, ng], BF16, "aT")
            m1 = ar.mark()
            ulo, uhi = max(t0 - 1, 0), t0 + ng + 1
            nU = uhi - ulo
            off0 = 1 if t0 == 0 else 0
            h2 = ar.alloc([P, DC, nU], BF16, "h2")
            stage_norm(xmid[l], ulo, uhi, h2, l, 1, NT=256, depth=3)
            wg = Ring([ar.alloc([P, DC, 256], BF16, f"wg{i}") for i in range(2)])
            wv = Ring([ar.alloc([P, DC, 256], BF16, f"wv{i}") for i in range(2)])
            ugr = Ring([ar.alloc([P, ng + 2], F32, f"ug{i}") for i in range(2)])
            uvr = Ring([ar.alloc([P, ng + 2], F32, f"uv{i}") for i in range(2)])
            cgr = Ring([ar.alloc([P, ng], F32, f"cg{i}") for i in range(2)])
            cvr = Ring([ar.alloc([P, ng], F32, f"cv{i}") for i in range(2)])
            pA = Ring(banks[0:8])
            if off0:
                for t in ugr.items + uvr.items:
                    pg.op("dve", lambda g, t=t: g.memset(t.ap[:, 0:1], 0.0), writes=[t])
            d1w = {}

            def d1_load(i_):
                if i_ >= FC // 2:
                    return
                a_, b_ = wg.next(), wv.next()
                pg.dma("pool", a_.ap, w_up[l, :, i_ * 256:(i_ + 1) * 256].rearrange("(k p) n -> p k n", p=P), writes=[a_])
                pg.dma("pool", b_.ap, w_up[l, :, DFF + i_ * 256:DFF + (i_ + 1) * 256].rearrange("(k p) n -> p k n", p=P), writes=[b_])
                d1w[i_] = (a_, b_)
            d1_load(0)
            for mg_ in range(FC // 2):
                d1_load(mg_ + 1)
                wgt, wvt = d1w.pop(mg_)
                c0 = mg_ * 256
                for ci in range(2):
                    m = mg_ * 2 + ci
                    rows = []
                    for (wt, rr, mc) in ((wgt, ugr, m), (wvt, uvr, FC + m)):
                        row = rr.next()
                        for (s, n) in tiles(0, nU, 512):
                            pa = pA.next()

                            def f(g, pa=pa, wt=wt, s=s, n=n, ci=ci):
                                for k in range(DC):
                                    ins = g.matmul(pa.ap[:, 0:n], wt.ap[:, k, ci * P:(ci + 1) * P], h2.ap[:, k, s:s + n], start=(k == 0), stop=(k == DC - 1))
                                return ins
                            pg.op("pe", f, reads=[wt, h2], writes=[pa])
                            pg.op("act", lambda g, row=row, pa=pa, s=s, n=n: g.copy(row.ap[:, off0 + s:off0 + s + n], pa.ap[:, 0:n]),
                                  reads=[pa], writes=[row])
                        rows.append((row, mc))
                    outs = []
                    for (row, mc), cr in zip(rows, (cgr, cvr)):
                        c = cr.next()
                        w0 = smc(l, O_CW + 0 * 88 + mc)
                        w1 = smc(l, O_CW + 1 * 88 + mc)
                        w2 = smc(l, O_CW + 2 * 88 + mc)
                        cb = smc(l, O_CB + mc)
                        pg.op("dve", lambda g, c=c, row=row, w1=w1, cb=cb: g.tensor_scalar(c.ap, row.ap[:, 1:ng + 1], w1, cb, ALU.mult, ALU.add),
                              reads=[row, sm], writes=[c])
                        pg.op("dve", lambda g, c=c, row=row, w0=w0: g.scalar_tensor_tensor(c.ap, row.ap[:, 0:ng], w0, c.ap, ALU.mult, ALU.add),
                              reads=[row, sm, c], writes=[c])
                        pg.op("dve", lambda g, c=c, row=row, w2=w2: g.scalar_tensor_tensor(c.ap, row.ap[:, 2:ng + 2], w2, c.ap, ALU.mult, ALU.add),
                              reads=[row, sm, c], writes=[c])
                        outs.append(c)
                    cg_, cv_ = outs
                    pg.op("act", lambda g, cg_=cg_: g.activation(out=cg_.ap, in_=cg_.ap, func=AF.Silu), reads=[cg_], writes=[cg_])
                    pg.op("dve", lambda g, cg_=cg_, cv_=cv_, m=m: g.tensor_tensor(aT.ap[:, m, :], cg_.ap, cv_.ap, ALU.mult),
                          reads=[cg_, cv_], writes=[aT])
            ar.reset(m1)
            wd = Ring([ar.alloc([P, FC, 256], BF16, f"wd{i}") for i in range(2)])
            xr = Ring([ar.alloc([P, 512], F32, f"dx{i}") for i in range(3)])
            xo = Ring([ar.alloc([P, 512], F32, f"dxo{i}") for i in range(3)])
            pA = Ring(banks[0:8])
            d2w = {}

            def d2_load(cg_):
                if cg_ >= 8:
                    return
                w_ = wd.next()
                pg.dma("pool", w_.ap, w_down[l, :, cg_ * 256:(cg_ + 1) * 256].rearrange("(k p) n -> p k n", p=P), writes=[w_])
                d2w[cg_] = w_
            d2_load(0)
            for cg in range(8):
                d2_load(cg + 1)
                wt = d2w.pop(cg)
                c0 = cg * 256
                for ci in range(2):
                    j = cg * 2 + ci
                    for (s, n) in tiles(0, ng, 512):
                        xt = xr.next()
                        pg.dma("sp", xt.ap[:, 0:n], xmid[l][j, :, t0 + s:t0 + s + n], writes=[xt])
                        pa = pA.next()

                        def f(g, pa=pa, wt=wt, ci=ci, s=s, n=n):
                            for k in range(FC):
                                ins = g.matmul(pa.ap[:, 0:n], wt.ap[:, k, ci * P:(ci + 1) * P], aT.ap[:, k, s:s + n], start=(k == 0), stop=(k == FC - 1))
                            return ins
                        pg.op("pe", f, reads=[wt, aT], writes=[pa])
                        xn = xo.next()
                        pg.op("dve", lambda g, xn=xn, pa=pa, xt=xt, j=j, n=n: g.scalar_tensor_tensor(xn.ap[:, 0:n], pa.ap[:, 0:n], Gm[:, j:j + 1], xt.ap[:, 0:n], ALU.mult, ALU.add),
                              reads=[pa, xt, der], writes=[xn])
                        lim = min(t0 + s + n, xdst[l].shape[2])
                        if lim > t0 + s:
                            pg.dma("act", xdst[l][j, :, t0 + s:lim], xn.ap[:, 0:lim - (t0 + s)], reads=[xn])
            ar.reset(m0)

        for (t0_, ng_) in tiles(0, tf, G3):
            group_d(t0_, ng_)
        pg.barrier()

    pg.barrier()
    order = []
    stage_ada()
    plan = [("b1", 0), ("b2", 0), ("c", 0), ("d", 0), ("b1", 1), ("b2", 1), ("c", 1), ("d", 1)]
    fns = {"b1": stage_b1, "b2": stage_b2, "c": stage_c, "d": stage_d}
    if upto != "ada":
        for (nm, l) in plan:
            fns[nm](l)
            if upto == f"{nm}{l}":
                break
    if "modfm" in dump:
        md = nc.dram_tensor("modfm_o", [P, NL * 96], F32, kind="ExternalOutput").ap()
        pg.dma("sp", md, modfm.ap.rearrange("p l c -> p (l c)"), reads=[modfm])
    pg.barrier()
    with nc.allow_non_contiguous_dma(reason="single-token (+1 conv halo) columns"):
        pg.emit()
    return nc, es


def _tables(mir):
    T = TKV[0]
    t = np.arange(T)
    pos = (SEQ - 1 - t) if mir else t
    inv = np.float32(500000.0) ** (-np.arange(0, 32, 2, dtype=np.float32) / 32)
    ang = pos.astype(np.float32)[None, :] * inv[:, None]
    ang = np.concatenate([ang, ang], axis=0)
    cs = np.stack([np.cos(ang), np.sin(ang)], axis=1).astype(np.float32)
    wm = np.zeros((P, 2, 3, P), np.float32)
    for ty in range(2):
        q = ty * P + np.arange(P)[None, :]
        for kb in range(3):
            k = kb * P + np.arange(P)[:, None]
            wm[:, ty, kb, :] = np.where(np.abs(q - k) <= 128, 0.0, NEG)
    return cs, wm


def _nbias(rel_bias_l, mir):
    out = np.full((HB, P, 3, 5, P), NEG, np.float32)
    for ty in range(3):
        q = ty * P + np.arange(P)
        for kb in range(5):
            k = kb * P + np.arange(P)
            gq = (SEQ - 1 - q) if mir else q
            gk = (SEQ - 1 - k) if mir else k
            Rq, Cq = gq // 64, gq % 64
            Rk, Ck = gk // 64, gk % 64
            rs = np.clip(Rq - 4, 0, 56)
            c0 = np.clip(Cq - 8, 0, 48)
            ok = ((Rk[:, None] >= rs[None, :]) & (Rk[:, None] < rs[None, :] + 8) &
                  (Ck[:, None] >= c0[None, :]) & (Ck[:, None] < c0[None, :] + 16))
            dr = np.clip(Rk[:, None] - Rq[None, :] + 7, 0, 14)
            dc = np.clip(Ck[:, None] - Cq[None, :] + 15, 0, 30)
            vals = rel_bias_l[:, dr, dc]
            out[:, :, ty, kb, :] = np.where(ok[None], vals, np.float32(NEG))
    return out


def _prep_core(c, x, cvec, norm_mix, norm_ffn, qn_a, kn_a, qn_b, kn_b, sink_a, rel_bias_b, conv_w, conv_b, shared):
    b, mir = c // 2, c % 2
    T = TKV[0]
    if mir:
        xl = x[b, SEQ - T:SEQ][::-1]
    else:
        xl = x[b, 0:T]
    xT = np.ascontiguousarray(xl.T).reshape(DC, P, T)
    sm = np.zeros((P, SM_N), np.float32)
    sm[:, SM_C:SM_C + 16] = cvec[b].reshape(16, P).T
    for l in range(NL):
        o = SM_L + l * SM_LSZ
        sm[:, o + O_NM:o + O_NM + 16] = norm_mix[l].reshape(16, P).T
        sm[:, o + O_NF:o + O_NF + 16] = norm_ffn[l].reshape(16, P).T
        sm[:, o + O_QNA] = qn_a[l]
        sm[:, o + O_KNA] = kn_a[l]
        sm[:, o + O_QNB] = qn_b[l]
        sm[:, o + O_KNB] = kn_b[l]
        sm[:, o + O_SINK:o + O_SINK + 8] = sink_a[l][None, :]
        cw = conv_w[l][::-1] if mir else conv_w[l]
        sm[:, o + O_CW:o + O_CW + 264] = cw.reshape(3, 88, P).transpose(2, 0, 1).reshape(P, 264)
        sm[:, o + O_CB:o + O_CB + 88] = conv_b[l].reshape(88, P).T
    cs, wm = shared["tab"][mir]
    nbias = shared["nb"][mir]
    m = dict(shared["w"])
    m.update({"xin": xT, "sm": sm, "cs": cs, "wmask": wm, "nbias": nbias})
    return m


_CACHE = {}


def _get_prog():
    if "nc" not in _CACHE:
        _CACHE["nc"] = build_program()
    return _CACHE["nc"][0]


def kernel(x, c, ada_w, ada_b, norm_mix, norm_ffn, w_in, qn_a, kn_a, qn_b, kn_b,
           sink_a, rel_bias_b, w_proj_a, w_proj_b, w_out, w_up, conv_w, conv_b, w_down):
    f = lambda a: np.ascontiguousarray(np.asarray(a, dtype=np.float32))
    x, c = f(x), f(c)
    norm_mix, norm_ffn, qn_a, kn_a, qn_b, kn_b = map(f, (norm_mix, norm_ffn, qn_a, kn_a, qn_b, kn_b))
    sink_a, rel_bias_b, conv_w, conv_b = map(f, (sink_a, rel_bias_b, conv_w, conv_b))
    rm = np.zeros((32, 32), np.float32)
    for m_ in range(16):
        rm[m_ + 16, m_] = -1.0
        rm[m_, m_ + 16] = 1.0
    shared = {
        "w": {"ada_w": f(ada_w), "ada_b": f(ada_b), "w_in": f(w_in), "w_proj_a": f(w_proj_a), "w_proj_b": f(w_proj_b),
              "w_out": f(w_out), "w_up": f(w_up), "w_down": f(w_down), "rmat": rm},
        "tab": [_tables(0), _tables(1)],
        "nb": [np.stack([_nbias(rel_bias_b[l], mir) for l in range(NL)]) for mir in range(2)],
    }
    in_maps = [_prep_core(cc, x, c, norm_mix, norm_ffn, qn_a, kn_a, qn_b, kn_b, sink_a, rel_bias_b, conv_w, conv_b, shared)
               for cc in range(8)]
    nc = _get_prog()
    res = run_bass_kernel_spmd(nc, in_maps, core_ids=list(range(8)))
    out = np.empty((4, SEQ, D), np.float32)
    for cc in range(8):
        o = np.asarray(res.results[cc]["out"]).reshape(D, 2048).T
        b, mir = cc // 2, cc % 2
        if mir:
            out[b, 2048:] = o[::-1]
        else:
            out[b, :2048] = o
    return out
```
